# Optimizing a Trainium2 kernel written in Bass

```python
import math
import jax, jax.numpy as jnp
from jax import lax
import numpy as np

D_MODEL = 2048
BATCH = 1
SEQ = 8192
DEPTH = 4

BRANCH_WIDTH = 1024
N_BRANCH = 3
DIL_PAIRS = ((128, 1), (512, 4), (2048, 16))
N_DIL_GROUPS = 3
A_HEADS = 8
A_HEAD_DIM = 128
HY_BANDS = 16
HY_EMB_DIM = 1 + 2 * HY_BANDS
HY_FFN = 64
HY_ORDER = 2
HY_SHORT = 3
HY_TARGET = 1e-2
HY_FAST_PCT = 0.3
HY_SLOW_PCT = 1.5
C_HEADS = 8
C_QK_DIM = 64
C_V_DIM = 128
Q_BLOCK = 128
N_BUCKETS = 32
REL_MAX_DIST = 1024
N_BIAS_HEADS = N_DIL_GROUPS * A_HEADS + C_HEADS
NORM_EPS = 1e-6
NEG = -1e30

A_QKV_COLS = N_DIL_GROUPS * 3 * A_HEADS * A_HEAD_DIM
B_IN_COLS = (HY_ORDER + 1) * BRANCH_WIDTH
C_QKV_COLS = 2 * 2 * C_HEADS * C_QK_DIM + C_HEADS * C_V_DIM
MERGE_COLS = N_BRANCH * D_MODEL
COL_SIZES = (A_QKV_COLS, BRANCH_WIDTH, B_IN_COLS, BRANCH_WIDTH, C_QKV_COLS, BRANCH_WIDTH, MERGE_COLS)
IN_COLS = int(sum(COL_SIZES))
SPLITS = [int(s) for s in np.cumsum(COL_SIZES)[:-1]]

kernel_name = "hybrid_dilated_hyena_diffattn_encoder"


def rms_norm(x, g):
    xf = x.astype(jnp.float32)
    y = xf * lax.rsqrt(jnp.mean(xf * xf, axis=-1, keepdims=True) + NORM_EPS)
    return (y * g.astype(jnp.float32)).astype(x.dtype)


def t5_bucket(rel):
    nb = N_BUCKETS // 2
    max_exact = nb // 2
    ret = (rel > 0).astype(jnp.int32) * nb
    n = jnp.abs(rel)
    nf = jnp.maximum(n, 1).astype(jnp.float32)
    large = max_exact + (jnp.log(nf / max_exact) / math.log(REL_MAX_DIST / max_exact) * (nb - max_exact)).astype(jnp.int32)
    large = jnp.minimum(large, nb - 1)
    return ret + jnp.where(n < max_exact, n, large)


def dilated_group(q, k, v, bias_tab, dil, reach):
    B, S, H, hd = q.shape
    n = S // dil
    blk = reach
    nb = -(-n // blk)
    n_pad = nb * blk

    def split_res(t):
        return t.reshape(B, n, dil, H, hd).transpose(0, 2, 3, 1, 4)

    qs, ks, vs = split_res(q), split_res(k), split_res(v)
    qb = jnp.pad(qs, ((0, 0), (0, 0), (0, 0), (0, n_pad - n), (0, 0))).reshape(B, dil, H, nb, blk, hd)

    def band(t):
        tp = jnp.pad(t, ((0, 0), (0, 0), (0, 0), (blk, n_pad - n + blk), (0, 0)))
        tb = tp.reshape(B, dil, H, nb + 2, blk, hd)
        return jnp.concatenate([tb[:, :, :, :-2], tb[:, :, :, 1:-1], tb[:, :, :, 2:]], axis=4)

    kb, vb = band(ks), band(vs)
    t_idx = jnp.arange(blk)[:, None]
    u_idx = jnp.arange(3 * blk)[None, :]
    rel = u_idx - blk - t_idx
    m_k = jnp.arange(nb)[:, None, None] * blk - blk + u_idx
    valid = (jnp.abs(rel) <= reach)[None] & (m_k >= 0) & (m_k < n)
    bias = bias_tab[t5_bucket(rel * dil)].astype(jnp.float32).transpose(2, 0, 1)
    scale = 1.0 / math.sqrt(hd)
    logits = jnp.einsum('bdhnqc,bdhnkc->bdhnqk', qb, kb, preferred_element_type=jnp.float32) * scale
    logits = jnp.where(valid[None, None, None], logits + bias[None, None, :, None], NEG)
    lse = jax.nn.logsumexp(logits, axis=-1)
    p = jnp.exp(logits - lse[..., None])
    o = jnp.einsum('bdhnqk,bdhnkc->bdhnqc', p.astype(v.dtype), vb)
    o = o.reshape(B, dil, H, n_pad, hd)[:, :, :, :n].transpose(0, 3, 1, 2, 4).reshape(B, S, H, hd)
    lse = lse.reshape(B, dil, H, n_pad)[..., :n].transpose(0, 3, 1, 2).reshape(B, S, H)
    return o, lse


def diff_attention(q, k, v, lam, bias_tab):
    B, S, H, _, dq = q.shape
    nqb = S // Q_BLOCK
    qb = q.reshape(B, nqb, Q_BLOCK, H, 2, dq).transpose(1, 0, 2, 3, 4, 5)
    kpos = jnp.arange(S)
    scale = 1.0 / math.sqrt(dq)

    def one_block(args):
        qblk, bi = args
        qpos = bi * Q_BLOCK + jnp.arange(Q_BLOCK)
        bias = bias_tab[t5_bucket(kpos[None, :] - qpos[:, None])].astype(jnp.float32).transpose(2, 0, 1)
        logits = jnp.einsum('bqhcd,bkhcd->bhcqk', qblk, k, preferred_element_type=jnp.float32) * scale
        p = jax.nn.softmax(logits + bias[None, :, None], axis=-1)
        a = p[:, :, 0] - lam * p[:, :, 1]
        return jnp.einsum('bhqk,bkhd->bqhd', a.astype(v.dtype), v)

    o = lax.map(one_block, (qb, jnp.arange(nqb)))
    return o.transpose(1, 0, 2, 3, 4).reshape(B, S, H, v.shape[-1])


def short_conv(u, w):
    C = u.shape[-1]
    pad = HY_SHORT // 2
    return lax.conv_general_dilated(u, w[:, None, :].astype(u.dtype), window_strides=(1,), padding=((pad, pad),),
                                    dimension_numbers=('NWC', 'WIO', 'NWC'), feature_group_count=C)


def hyena_filters(L, w1, b1, freq, w2, b2, w3):
    f32 = jnp.float32
    t = jnp.linspace(0.0, 1.0, L, dtype=f32)[:, None]
    w = 2.0 * math.pi * jnp.arange(L, dtype=f32)[:, None] / L
    fb = jnp.linspace(1e-4, HY_BANDS - 1, HY_BANDS, dtype=f32)[None]
    z = jnp.concatenate([t, jnp.cos(fb * w), -jnp.sin(fb * w)], axis=-1)
    freq = freq.astype(f32)
    hid = jnp.sin(freq[0] * (z @ w1.astype(f32) + b1.astype(f32)))
    hid = jnp.sin(freq[1] * (hid @ w2.astype(f32) + b2.astype(f32)))
    h = (hid @ w3.astype(f32)).reshape(L, HY_ORDER, 2, BRANCH_WIDTH)
    max_decay = math.log(HY_TARGET) / HY_FAST_PCT
    min_decay = math.log(HY_TARGET) / HY_SLOW_PCT
    deltas = jnp.linspace(min_decay, max_decay, BRANCH_WIDTH, dtype=f32)
    decay = jnp.exp(-t * jnp.abs(deltas)[None])
    return h * decay[:, None, None, :]


def fft_conv(z, h):
    L = z.shape[1]
    n = 2 * L
    Z = jnp.fft.rfft(z, n=n, axis=1)
    Hf = jnp.fft.rfft(h, n=n, axis=0)
    return jnp.fft.irfft(Z * Hf[None], n=n, axis=1)[:, :L]


def bidir_long_conv(z, h_fwd, h_bwd, skip):
    zf = z.astype(jnp.float32)
    y = fft_conv(zf, h_fwd) + jnp.flip(fft_conv(jnp.flip(zf, 1), h_bwd), 1) + skip.astype(jnp.float32) * zf
    return y.astype(z.dtype)


def setup_inputs(seed: int = 0) -> dict:
    key = jax.random.key(seed)
    ks = jax.random.split(key, 20)
    f32 = jnp.float32
    nrm = lambda k, shape, s: jax.random.normal(k, shape, f32) * s
    return {
        "x": nrm(ks[0], (BATCH, SEQ, D_MODEL), 1.0),
        "norm_g": 1.0 + nrm(ks[1], (DEPTH, D_MODEL), 0.02),
        "final_g": 1.0 + nrm(ks[2], (D_MODEL,), 0.02),
        "w_in": nrm(ks[3], (DEPTH, D_MODEL, IN_COLS), D_MODEL ** -0.5),
        "merge_b": nrm(ks[4], (DEPTH, N_BRANCH, D_MODEL), 0.02),
        "rel_bias": nrm(ks[5], (N_BUCKETS, N_BIAS_HEADS), 0.2),
        "hy_conv": nrm(ks[6], (DEPTH, HY_SHORT, B_IN_COLS), HY_SHORT ** -0.5),
        "hy_w1": nrm(ks[7], (DEPTH, HY_EMB_DIM, HY_FFN), HY_EMB_DIM ** -0.5),
        "hy_b1": nrm(ks[8], (DEPTH, HY_FFN), 0.1),
        "hy_freq": 1.0 + nrm(ks[9], (DEPTH, 2, HY_FFN), 0.05),
        "hy_w2": nrm(ks[10], (DEPTH, HY_FFN, HY_FFN), HY_FFN ** -0.5),
        "hy_b2": nrm(ks[11], (DEPTH, HY_FFN), 0.1),
        "hy_w3": nrm(ks[12], (DEPTH, HY_FFN, HY_ORDER * 2 * BRANCH_WIDTH), 0.05 * HY_FFN ** -0.5),
        "hy_skip": nrm(ks[13], (DEPTH, HY_ORDER, BRANCH_WIDTH), 0.5),
        "diff_lam": nrm(ks[14], (DEPTH, 4, C_QK_DIM), 0.1),
        "diff_g": 1.0 + nrm(ks[15], (DEPTH, C_V_DIM), 0.02),
        "w_proj": nrm(ks[16], (DEPTH, N_BRANCH, BRANCH_WIDTH, D_MODEL), BRANCH_WIDTH ** -0.5),
        "w_out": nrm(ks[17], (DEPTH, D_MODEL, D_MODEL), D_MODEL ** -0.5),
    }


def reference(x, norm_g, final_g, w_in, merge_b, rel_bias, hy_conv, hy_w1, hy_b1, hy_freq, hy_w2, hy_b2, hy_w3,
              hy_skip, diff_lam, diff_g, w_proj, w_out):
    B, S, _ = x.shape
    W = BRANCH_WIDTH
    for l in range(DEPTH):
        h = rms_norm(x, norm_g[l])
        proj = h @ w_in[l]
        a_qkv, a_gate, b_in, b_gate, c_qkv, c_gate, merge = jnp.split(proj, SPLITS, axis=-1)

        a_qkv = a_qkv.reshape(B, S, N_DIL_GROUPS, 3, A_HEADS, A_HEAD_DIM)
        outs, lses = [], []
        for g, (win, dil) in enumerate(DIL_PAIRS):
            o_g, lse_g = dilated_group(a_qkv[:, :, g, 0], a_qkv[:, :, g, 1], a_qkv[:, :, g, 2],
                                       rel_bias[:, g * A_HEADS:(g + 1) * A_HEADS], dil, win // (2 * dil))
            outs.append(o_g)
            lses.append(lse_g)
        wts = jax.nn.softmax(jnp.stack(lses), axis=0)
        a_out = jnp.einsum('gbsh,gbshd->bshd', wts, jnp.stack(outs).astype(jnp.float32)).astype(x.dtype)
        a_out = a_out.reshape(B, S, W) * jax.nn.silu(a_gate)

        u = short_conv(b_in, hy_conv[l])
        v, x1, x2 = jnp.split(u, 3, axis=-1)
        filt = hyena_filters(S, hy_w1[l], hy_b1[l], hy_freq[l], hy_w2[l], hy_b2[l], hy_w3[l])
        z = v
        for o, gate in enumerate((x1, x2)):
            z = gate * bidir_long_conv(z, filt[:, o, 0], filt[:, o, 1], hy_skip[l, o])
        b_out = z * jax.nn.silu(b_gate)

        qk_w = 2 * C_HEADS * C_QK_DIM
        cq = c_qkv[..., :qk_w].reshape(B, S, C_HEADS, 2, C_QK_DIM)
        ck = c_qkv[..., qk_w:2 * qk_w].reshape(B, S, C_HEADS, 2, C_QK_DIM)
        cv = c_qkv[..., 2 * qk_w:].reshape(B, S, C_HEADS, C_V_DIM)
        lam_init = 0.8 - 0.6 * math.exp(-0.3 * l)
        dl = diff_lam[l].astype(jnp.float32)
        lam = jnp.exp(jnp.sum(dl[0] * dl[1])) - jnp.exp(jnp.sum(dl[2] * dl[3])) + lam_init
        c_o = diff_attention(cq, ck, cv, lam, rel_bias[:, N_DIL_GROUPS * A_HEADS:])
        c_o = rms_norm(c_o, diff_g[l]) * (1.0 - lam_init)
        c_out = c_o.reshape(B, S, W) * jax.nn.silu(c_gate)

        branches = jnp.stack([a_out, b_out, c_out], axis=2)
        proj_b = jnp.einsum('bsnw,nwd->bsnd', branches, w_proj[l])
        gates = jax.nn.sigmoid(merge.reshape(B, S, N_BRANCH, D_MODEL) + merge_b[l])
        y = jnp.sum(gates * proj_b, axis=2)
        x = x + y @ w_out[l]
    return rms_norm(x, final_g)
```

```python
import math
import numpy as np
import ml_dtypes
import concourse.bass as bass
import concourse.mybir as mybir
from concourse.bass_utils import run_bass_kernel_spmd

F32 = mybir.dt.float32
BF16 = mybir.dt.bfloat16
AF = mybir.ActivationFunctionType
ALU = mybir.AluOpType
AX = mybir.AxisListType

PAGE = 20000


class Res:
    __slots__ = ("name", "w", "r", "dsem", "dcnt")

    def __init__(self, name):
        self.name = name
        self.w = None
        self.r = {}
        self.dsem = None
        self.dcnt = 0


class K:
    ENG = ("pe", "act", "dve", "pool", "sp")

    def __init__(self, nc):
        self.nc = nc
        self.e = {"pe": nc.tensor, "act": nc.scalar, "dve": nc.vector, "pool": nc.gpsimd, "sp": nc.sync}
        self.sem = {}
        self.cnt = {}
        self.page = {}
        self.waited = {}
        self.nsem = 0
        self.sem_owner = {}
        self.recycle = False
        self.dpool = []
        self.dlive = []
        for n in self.ENG:
            self.cnt[n] = 0
            self.page[n] = 0
            self.sem[n] = self.new_sem(f"s_{n}_0", n)
        self.dma_events = {}
        self.all_sems = []
        self.q = {n: [] for n in self.ENG}

    def new_sem(self, name, owner=None):
        self.nsem += 1
        sem = self.nc.alloc_semaphore(f"{name}_{self.nsem}")
        if owner is not None:
            self.sem_owner[id(sem)] = owner
        return sem

    def _wait(self, eng, ev):
        if ev is None:
            return
        sem, val = ev
        if eng == "pe" and self.sem_owner.get(id(sem)) == "pe":
            return
        key = (eng, id(sem))
        if self.waited.get(key, 0) >= val:
            return
        self.waited[key] = val
        self.q[eng].append(("w", sem, val))

    def _deps(self, eng, reads, writes):
        for r in reads:
            if hasattr(r, "wall"):
                for ev in r.wall.values():
                    self._wait(eng, ev)
            else:
                self._wait(eng, r.w)
        for w in writes:
            if not hasattr(w, "wall"):
                self._wait(eng, w.w)
            for ev in w.r.values():
                self._wait(eng, ev)

    def _commit(self, ev, reads, writes):
        for r in reads:
            old = r.r.get(id(ev[0]))
            if old is None or old[1] < ev[1]:
                r.r[id(ev[0])] = ev
        for w in writes:
            if hasattr(w, "wall"):
                old = w.wall.get(id(ev[0]))
                if old is None or old[1] < ev[1]:
                    w.wall[id(ev[0])] = ev
            else:
                w.w = ev
            w.r = {}

    def op(self, eng, fn, reads=(), writes=(), sig=True):
        self._deps(eng, reads, writes)
        if sig:
            if self.cnt[eng] >= PAGE:
                self.sem[eng] = self.new_sem(f"s_{eng}", eng)
                self.cnt[eng] = 0
            self.cnt[eng] += 1
            self.q[eng].append(("i", fn, self.sem[eng], 1))
            ev = (self.sem[eng], self.cnt[eng])
        else:
            if self.cnt[eng] >= PAGE:
                self.sem[eng] = self.new_sem(f"s_{eng}", eng)
                self.cnt[eng] = 0
            self.q[eng].append(("i", fn, None, 0))
            ev = (self.sem[eng], self.cnt[eng] + 1)
        self._commit(ev, reads, writes)

    def dma(self, eng, out, in_, reads=(), writes=(), owner=None, **kw):
        self._deps(eng, reads, writes)
        if owner is None:
            owner = writes[0]
        self._dsem(owner)
        self.q[eng].append(("i", lambda e: e.dma_start(out=out, in_=in_, **kw), owner.dsem, 16))
        owner.dcnt += 16
        ev = (owner.dsem, owner.dcnt)
        self.dma_events[id(owner.dsem)] = ev
        self._commit(ev, reads, writes)

    def _dsem(self, owner):
        if owner.dsem is None:
            if self.dpool:
                owner.dsem, owner.dcnt = self.dpool.pop()
            else:
                owner.dsem = self.new_sem("d")
            self.dlive.append(owner)

    def custom(self, eng, fn, inc, reads=(), writes=(), owner=None):
        self._deps(eng, reads, writes)
        self._dsem(owner)
        self.q[eng].append(("i", fn, owner.dsem, inc))
        owner.dcnt += inc
        ev = (owner.dsem, owner.dcnt)
        self.dma_events[id(owner.dsem)] = ev
        self._commit(ev, reads, writes)

    def barrier(self, engines=None):
        engines = engines or self.ENG
        evs = [(self.sem[s], self.cnt[s]) for s in self.ENG if self.cnt[s] > 0]
        evs += list(self.dma_events.values())
        for eng in engines:
            for ev in evs:
                self._wait(eng, ev)
        self.dma_events = {}
        if self.recycle and len(engines) == len(self.ENG):
            for r in self.dlive:
                self.dpool.append((r.dsem, r.dcnt))
                r.dsem = None
            self.dlive = []

    def emit(self):
        self.barrier()
        nc = self.nc
        q = self.q

        def run(e, items):
            for it in items:
                if it[0] == "w":
                    e.wait_ge(it[1], it[2])
                else:
                    ins = it[1](e)
                    if it[2] is not None:
                        ins.then_inc(it[2], it[3])

        with nc.Block() as block:
            @block.tensor
            def _(e):
                run(e, q["pe"])

            @block.scalar
            def _(e):
                run(e, q["act"])

            @block.vector
            def _(e):
                run(e, q["dve"])

            @block.gpsimd
            def _(e):
                run(e, q["pool"])

            @block.sync
            def _(e):
                run(e, q["sp"])

    def mm(self, out, lhsT, rhs, start, stop, reads, writes):
        self.op("pe", lambda e: e.matmul(out, lhsT, rhs, start=start, stop=stop), reads, writes, sig=stop)

    def act(self, out, in_, func, reads, writes, bias=None, scale=None):
        kw = {}
        if bias is not None:
            kw["bias"] = bias
        if scale is not None:
            kw["scale"] = scale
        self.op("act", lambda e: e.activation(out=out, in_=in_, func=func, **kw), reads, writes)

    def copy(self, eng, out, in_, reads, writes):
        if eng == "act":
            self.op("act", lambda e: e.copy(out, in_), reads, writes)
        else:
            self.op(eng, lambda e: e.tensor_copy(out, in_), reads, writes)

    def tt(self, eng, out, in0, in1, op, reads, writes):
        self.op(eng, lambda e: e.tensor_tensor(out, in0, in1, op), reads, writes)

    def ts(self, eng, out, in0, s1, s2, op0, op1, reads, writes):
        if s2 is None:
            self.op(eng, lambda e: e.tensor_scalar(out, in0, s1, None, op0), reads, writes)
        else:
            self.op(eng, lambda e: e.tensor_scalar(out, in0, s1, s2, op0, op1), reads, writes)

    def stt(self, eng, out, in0, scalar, in1, op0, op1, reads, writes):
        self.op(eng, lambda e: e.scalar_tensor_tensor(out, in0, scalar, in1, op0, op1), reads, writes)

    def memset(self, eng, ap, val, writes):
        self.op(eng, lambda e: e.memset(ap, val), (), writes)


class PsumPool:
    def __init__(self, nc, n=8):
        self.t = [nc.alloc_psum_tensor(f"psb{i}", [128, 512], F32) for i in range(n)]
        self.r = [Res(f"psb{i}") for i in range(n)]
        self.i = 0
        self.n = n

    def get(self):
        i = self.i
        self.i = (self.i + 1) % self.n
        return self.t[i], self.r[i]


class DRes(Res):
    __slots__ = ("wall",)

    def __init__(self, name):
        super().__init__(name)
        self.wall = {}


def _prod(x):
    r = 1
    for v in x:
        r *= v
    return r


class Arena:
    def __init__(self, nc, nbytes=204800):
        self.cap = nbytes
        self.t = nc.alloc_sbuf_tensor("arena", [128, nbytes // 4], F32)
        self.off = 0
        self.n = 0

    def reset(self):
        self.off = 0

    def alloc(self, name, free_shape, dtype):
        esz = 2 if dtype == BF16 else 4
        n = _prod(free_shape)
        nb = (n * esz + 31) // 32 * 32
        assert self.off + nb <= self.cap, f"arena overflow at {name}: {self.off + nb}"
        a = self.t[:, self.off // 4:(self.off + nb) // 4]
        self.off += nb
        if dtype != F32:
            a = a.bitcast(dtype)
        a = a[:, 0:n]
        if len(free_shape) == 2:
            a = a.rearrange("p (a b) -> p a b", a=free_shape[0])
        elif len(free_shape) == 3:
            a = a.rearrange("p (a b c) -> p a b c", a=free_shape[0], b=free_shape[1])
        self.n += 1
        return a, Res(f"{name}_{self.n}")


SCALE_A = 1.0 / math.sqrt(128.0)
SCALE_C = 1.0 / math.sqrt(64.0)
NEG_MASK = -30000.0


def stage_P(k, ar, pp, S, d):
    ar.reset()
    NT = S // 512
    hT, RhTd = d["hT"]
    W, RW = d["W"]
    WB, RWB = d["WB"]
    convw, Rcv = d["convw"]
    FMd, RFM = d["FMd"]
    TM1d, RT1 = d["TM1d"]
    TM2d, RT2 = d["TM2d"]
    Bd, RBd = d["Bd"]
    wsb, _ = ar.alloc("wsb", [16, 1920], BF16)
    Rw = [Res(f"wsb{q}") for q in range(4)]
    wB3, _ = ar.alloc("wB3", [3, 16, 384], BF16)
    RwB = [Res(f"wB3{q}") for q in range(4)]
    wBf, RwBf = ar.alloc("wBf", [4, 384], F32)
    cw, Rcw = ar.alloc("cw", [3, 384], F32)
    hTt = [ar.alloc(f"hTt{i}", [16, 514], BF16) for i in range(2)]
    stF = [ar.alloc(f"stF{i}", [8, 512], BF16)[0] for i in range(2)]
    RstF = [[Res(f"stF{i}_{b}") for b in range(8)] for i in range(2)]
    stT1 = [ar.alloc(f"stT1{i}", [4, 512], BF16)[0] for i in range(2)]
    RstT1 = [[Res(f"stT1{i}_{b}") for b in range(4)] for i in range(2)]
    stT2 = [ar.alloc(f"stT2{i}", [4, 384], F32)[0] for i in range(2)]
    RstT2 = [[Res(f"stT2{i}_{b}") for b in range(4)] for i in range(2)]
    stB = [ar.alloc(f"stB{i}", [4, 384], F32)[0] for i in range(2)]
    RstB = [[Res(f"stB{i}_{b}") for b in range(4)] for i in range(2)]

    Wv = W.rearrange("(kc p) c -> p kc c", p=128)
    WBv = WB.rearrange("(kc p) c -> p kc c", p=128)
    hTv = hT.rearrange("(kc p) s -> p kc s", p=128)
    for q in range(4):
        k.dma("pool", wsb[:, 4 * q:4 * q + 4, :], Wv[:, 4 * q:4 * q + 4, :], reads=[RW], writes=[Rw[q]])
    k.dma("sp", cw, convw.partition_broadcast(128), reads=[Rcv], writes=[Rcw])
    for q in range(4):
        k.dma("sp", wBf, WBv[:, 4 * q:4 * q + 4, :], reads=[RWB], writes=[RwBf])
        for ksh in range(3):
            k.tt("dve", wB3[:, ksh, 4 * q:4 * q + 4, :], wBf, cw[:, ksh:ksh + 1, :].to_broadcast([128, 4, 384]),
                 ALU.mult, reads=[RwBf, Rcw], writes=[RwB[q]])

    def load(i):
        b = i % 2
        t, r = hTt[b]
        c0 = i * 512 - 1
        c1 = i * 512 + 513
        lo, hi = 0, 514
        if i == 0:
            k.memset("pool", t[:, :, 0:1], 0.0, writes=[r])
            c0, lo = 0, 1
        if i == NT - 1:
            k.memset("pool", t[:, :, 513:514], 0.0, writes=[r])
            c1, hi = S, 513
        k.dma("sp", t[:, :, lo:hi], hTv[:, :, c0:c1], reads=[RhTd], writes=[r])

    load(0)
    ev = 0
    for i in range(NT):
        b = i % 2
        if i + 1 < NT:
            load(i + 1)
        t, r = hTt[b]
        for blk in range(8):
            ps, Rps = pp.get()
            for kc in range(16):
                k.mm(ps[:, :], wsb[:, kc, blk * 128:(blk + 1) * 128], t[:, kc, 1:513], kc == 0, kc == 15,
                     reads=[Rw[kc // 4], r], writes=[Rps])
            k.copy("act" if ev % 2 == 0 else "dve", stF[b][:, blk, :], ps[:, :], reads=[Rps], writes=[RstF[b][blk]])
            ev += 1
        k.dma("sp", FMd[:, :, i * 512:(i + 1) * 512].rearrange("b p s -> p b s"), stF[b],
              reads=RstF[b], writes=[RFM], owner=RstF[b][0])
        for sub in range(4):
            ps, Rps = pp.get()
            for kc in range(16):
                k.mm(ps[:, :], t[:, kc, 1 + sub * 128:129 + sub * 128], wsb[:, kc, 1024:1536], kc == 0, kc == 15,
                     reads=[Rw[kc // 4], r], writes=[Rps])
            k.copy("act" if ev % 2 == 0 else "dve", stT1[b][:, sub, :], ps[:, :], reads=[Rps], writes=[RstT1[b][sub]])
            ev += 1
            ps, Rps = pp.get()
            for kc in range(16):
                k.mm(ps[:, 0:384], t[:, kc, 1 + sub * 128:129 + sub * 128], wsb[:, kc, 1536:1920], kc == 0, kc == 15,
                     reads=[Rw[kc // 4], r], writes=[Rps])
            k.copy("act" if ev % 2 == 0 else "dve", stT2[b][:, sub, :], ps[:, 0:384], reads=[Rps], writes=[RstT2[b][sub]])
            ev += 1
            ps, Rps = pp.get()
            for ksh in range(3):
                for kc in range(16):
                    k.mm(ps[:, 0:384], t[:, kc, ksh + sub * 128:ksh + sub * 128 + 128], wB3[:, ksh, kc, :],
                         ksh == 0 and kc == 0, ksh == 2 and kc == 15, reads=[RwB[kc // 4], r], writes=[Rps])
            k.copy("act" if ev % 2 == 0 else "dve", stB[b][:, sub, :], ps[:, 0:384], reads=[Rps], writes=[RstB[b][sub]])
            ev += 1
        rows = slice(i * 512, (i + 1) * 512)
        k.dma("sp", TM1d[rows, :].rearrange("(sub p) c -> p sub c", p=128), stT1[b], reads=RstT1[b], writes=[RT1], owner=RstT1[b][0])
        k.dma("sp", TM2d[rows, :].rearrange("(sub p) c -> p sub c", p=128), stT2[b], reads=RstT2[b], writes=[RT2], owner=RstT2[b][0])
        k.dma("sp", Bd[rows, :].rearrange("(sub p) c -> p sub c", p=128), stB[b], reads=RstB[b], writes=[RBd], owner=RstB[b][0])
    k.barrier()


def stage_A(k, ar, pp, S, d):
    FMd, RFM = d["FMd"]
    TM1d, RT1 = d["TM1d"]
    TM2d, RT2 = d["TM2d"]
    BMA, RBMA = d["BMA"]
    OG, ROG = d["OG"]
    brA, RbrA = d["brA"]
    ar.reset()
    bm, Rbm = ar.alloc("bm", [9, 256], F32)
    k.dma("sp", bm, BMA, reads=[RBMA], writes=[Rbm])
    QTn, RQn = ar.alloc("QTn", [S], BF16)
    KTn, RKn = ar.alloc("KTn", [S], BF16)
    QTp, RQp = ar.alloc("QTp", [S], BF16)
    KTp, RKp = ar.alloc("KTp", [S + 16 * 128], BF16)
    VP, RVP = ar.alloc("VP", [S // 128 + 16, 130], BF16)
    Ost = [ar.alloc(f"Ost{i}", [64, 130], F32) for i in range(2)]
    tmp = [ar.alloc(f"tmpA{i}", [512], F32) for i in range(2)]
    PT = [ar.alloc(f"PT{i}", [512], BF16) for i in range(2)]
    it = 0
    for g, dil in enumerate((1, 4, 16)):
        n = S // dil
        J = n // 128
        k.dma("sp", QTn, FMd[2 * g, :, :], reads=[RFM], writes=[RQn])
        k.dma("sp", KTn, FMd[2 * g + 1, :, :], reads=[RFM], writes=[RKn])
        QTpv = QTp.rearrange("p (r m) -> p r m", r=dil)
        KTpv = KTp[:, 0:dil * (n + 128)].rearrange("p (r m) -> p r m", r=dil)
        VPv = VP[:, 0:dil * (J + 1), :].rearrange("p (r j) c -> p r j c", r=dil)
        k.memset("pool", KTp, 0.0, writes=[RKp])
        k.memset("pool", VP, 0.0, writes=[RVP])
        k.memset("pool", VP[:, :, 128:129], 1.0, writes=[RVP])
        Qsrc = QTn.rearrange("p (m r) -> p r m", r=dil)
        Ksrc = KTn.rearrange("p (m r) -> p r m", r=dil)
        for r in range(dil):
            k.copy("pool" if r % 2 == 0 else "dve", QTpv[:, r, :], Qsrc[:, r, :], reads=[RQn], writes=[RQp])
            k.copy("dve" if r % 2 == 0 else "pool", KTpv[:, r, 64:64 + n], Ksrc[:, r, :], reads=[RKn], writes=[RKp])
        Vsrc = TM1d[:, g * 128:(g + 1) * 128].rearrange("(m r) c -> r m c", r=dil)
        for r in range(dil):
            if J > 1:
                k.dma("sp", VPv[:, r, 1:J, 0:128],
                      Vsrc[r, 64:64 + 128 * (J - 1), :].rearrange("(j p) c -> p j c", p=128),
                      reads=[RT1], writes=[RVP])
            k.dma("sp", VPv[64:128, r, 0, 0:128], Vsrc[r, 0:64, :], reads=[RT1], writes=[RVP])
            k.dma("sp", VPv[0:64, r, J, 0:128], Vsrc[r, n - 64:n, :], reads=[RT1], writes=[RVP])
        OGv = OG[g].rearrange("(j q r) c -> r q j c", q=128, r=dil)
        for r in range(dil):
            ost, Rost = Ost[r % 2]
            for j0 in range(0, J, 2):
                ps, Rps = pp.get()
                tp, Rtp = tmp[it % 2]
                pt, Rpt = PT[it % 2]
                it += 1
                for jj in range(2):
                    j = j0 + jj
                    q = QTpv[:, r, 128 * j:128 * j + 128]
                    k.mm(ps[:, 256 * jj:256 * jj + 128], KTpv[:, r, 128 * j:128 * j + 128], q, True, True,
                         reads=[RKp, RQp], writes=[Rps])
                    k.mm(ps[:, 256 * jj + 128:256 * jj + 256], KTpv[:, r, 128 * (j + 1):128 * (j + 2)], q, True, True,
                         reads=[RKp, RQp], writes=[Rps])
                    var = 0 if j == 0 else (2 if j == J - 1 else 1)
                    k.stt("dve", tp[:, 256 * jj:256 * jj + 256], ps[:, 256 * jj:256 * jj + 256], SCALE_A,
                          bm[:, g * 3 + var, :], ALU.mult, ALU.add, reads=[Rps, Rbm], writes=[Rtp])
                k.act(pt, tp, AF.Exp, reads=[Rtp], writes=[Rpt])
                for jj in range(2):
                    j = j0 + jj
                    po, Rpo = pp.get()
                    k.mm(po[:, 0:130], pt[:, 256 * jj:256 * jj + 128], VPv[:, r, j, :], True, False,
                         reads=[Rpt, RVP], writes=[Rpo])
                    k.mm(po[:, 0:130], pt[:, 256 * jj + 128:256 * jj + 256], VPv[:, r, j + 1, :], False, True,
                         reads=[Rpt, RVP], writes=[Rpo])
                    k.copy("act", ost[:, j, :], po[:, 0:130], reads=[Rpo], writes=[Rost])
            k.dma("sp", OGv[r, :, :, :], ost[:, 0:J, :], reads=[Rost], writes=[ROG], owner=Rost)
    k.barrier()
    ar.reset()
    NB = 8
    ogs = [ar.alloc(f"ogs{i}", [3, NB, 130], F32) for i in range(2)]
    gt = [ar.alloc(f"gt{i}", [NB, 128], F32) for i in range(2)]
    acc = [ar.alloc(f"accA{i}", [NB, 130], F32) for i in range(2)]
    rec = [ar.alloc(f"recA{i}", [NB, 1], F32) for i in range(2)]
    ob = [ar.alloc(f"obA{i}", [NB, 128], BF16) for i in range(2)]
    OGn = OG.rearrange("g (t p) c -> p g t c", p=128)
    Gn = TM2d[:, 0:128].rearrange("(t p) c -> p t c", p=128)
    On = brA.rearrange("(t p) c -> p t c", p=128)
    for c in range(S // 128 // NB):
        b = c % 2
        o, Ro = ogs[b]
        gg, Rg = gt[b]
        a, Ra = acc[b]
        rc, Rr = rec[b]
        oo, Roo = ob[b]
        ts_ = slice(c * NB, (c + 1) * NB)
        for g in range(3):
            k.dma("sp", o[:, g, :, :], OGn[:, g, ts_, :], reads=[ROG], writes=[Ro])
        k.dma("sp", gg, Gn[:, ts_, :], reads=[RT2], writes=[Rg])
        k.tt("dve", a, o[:, 0, :, :], o[:, 1, :, :], ALU.add, reads=[Ro], writes=[Ra])
        k.tt("dve", a, a, o[:, 2, :, :], ALU.add, reads=[Ro, Ra], writes=[Ra])
        k.op("dve", lambda e, rc=rc, a=a: e.reciprocal(rc, a[:, :, 128:129]), reads=[Ra], writes=[Rr])
        k.act(gg, gg, AF.Silu, reads=[Rg], writes=[Rg])
        k.tt("dve", a[:, :, 0:128], a[:, :, 0:128], rc.to_broadcast([128, NB, 128]), ALU.mult, reads=[Ra, Rr], writes=[Ra])
        k.tt("dve", oo, a[:, :, 0:128], gg, ALU.mult, reads=[Ra, Rg], writes=[Roo])
        k.dma("sp", On[:, ts_, :], oo, reads=[Roo], writes=[RbrA], owner=Roo)
    k.barrier()


def _t5_bucket_np(rel):
    nb = 16
    max_exact = 8
    rel = np.asarray(rel, np.int64)
    ret = (rel > 0).astype(np.int64) * nb
    n = np.abs(rel)
    nf = np.maximum(n, 1).astype(np.float32)
    large = max_exact + (np.log(nf / np.float32(max_exact)) / np.float32(math.log(1024 / max_exact))
                         * np.float32(nb - max_exact)).astype(np.int64)
    large = np.minimum(large, nb - 1)
    return ret + np.where(n < max_exact, n, large)


def _bma_index():
    kp = np.arange(128)[:, None]
    qf = np.arange(128)[None, :]
    idx = np.zeros((128, 9, 256), np.int64)
    for g, dil in enumerate((1, 4, 16)):
        relA = kp - 64 - qf
        relB = kp + 64 - qf
        bA = _t5_bucket_np(relA * dil)
        bB = _t5_bucket_np(relB * dil)
        vA = kp >= qf
        vB = kp <= qf
        for var in range(3):
            va = vA & (kp >= 64) if var == 0 else vA
            vb = vB & (kp < 64) if var == 2 else vB
            idx[:, g * 3 + var, 0:128] = np.where(va, bA, 32)
            idx[:, g * 3 + var, 128:256] = np.where(vb, bB, 32)
    return idx


def build_L1(S, stages=("P", "A", "B", "C"), debug=False):
    nc = bass.Bass("TRN2", target_bir_lowering=False)
    k = K(nc)
    ar = Arena(nc)
    pp = PsumPool(nc)
    d = {}

    def ext_in(name, shape, dt):
        d[name] = (nc.dram_tensor(name, shape, dt, kind="ExternalInput").ap(), DRes(name))

    def scratch(name, shape, dt, out=False):
        kind = "ExternalOutput" if out else "Internal"
        d[name] = (nc.dram_tensor(name, shape, dt, kind=kind).ap(), DRes(name))

    ext_in("hT", [2048, S], BF16)
    ext_in("W", [2048, 1920], F32)
    ext_in("WB", [2048, 384], F32)
    ext_in("convw", [3, 384], F32)
    ext_in("BMA", [128, 9, 256], F32)
    scratch("FMd", [8, 128, S], BF16, out=debug)
    scratch("TM1d", [S, 512], BF16, out=debug)
    scratch("TM2d", [S, 384], F32, out=debug)
    scratch("Bd", [S, 384], F32, out=debug)
    scratch("OG", [3, S, 130], F32)
    scratch("brA", [S, 128], BF16, out=True)
    ext_in("Zx", [33, 2 * S], F32)
    ext_in("tpos", [1, 2 * S], F32)
    ext_in("hw1", [33, 64], F32)
    ext_in("hw2", [64, 64], F32)
    ext_in("hw3", [64, 512], F32)
    ext_in("hyp", [64, 4], F32)
    ext_in("hsk", [128, 3], F32)
    ext_in("Jrev", [128, 128], BF16)
    scratch("hx", [2, 128, 2 * S], BF16)
    scratch("brB", [S, 128], BF16, out=True)
    ext_in("BC", [128, 17, 128], F32)
    ext_in("cpar", [128, 388], F32)
    scratch("brC", [S, 128], BF16, out=True)
    if "P" in stages:
        stage_P(k, ar, pp, S, d)
    if "A" in stages:
        stage_A(k, ar, pp, S, d)
    if "B" in stages:
        stage_B(k, ar, pp, S, d)
    if "C" in stages:
        stage_C(k, ar, pp, S, d)
    k.emit()
    return nc


SPL = [9216, 10240, 13312, 14336, 17408, 18432]


def hyena_consts(S):
    f32 = np.float32
    L = S
    t = np.linspace(0.0, 1.0, L, dtype=f32)
    w = (f32(2.0 * math.pi) * np.arange(L, dtype=f32)) / f32(L)
    fb = np.linspace(1e-4, 15, 16, dtype=f32)
    z = np.concatenate([t[:, None], np.cos(fb[None] * w[:, None]), -np.sin(fb[None] * w[:, None])], axis=-1).astype(f32)
    pos = np.minimum(np.abs(np.arange(2 * S) - (S - 1)), L - 1)
    Zx = np.ascontiguousarray(z[pos].T)
    tpos = np.ascontiguousarray(t[pos][None, :])
    max_decay = math.log(1e-2) / 0.3
    min_decay = math.log(1e-2) / 1.5
    deltas = np.abs(np.linspace(min_decay, max_decay, 1024, dtype=f32))
    return Zx, tpos, deltas


def l1_inputs(inp, l, hd, S, consts):
    Zx, tpos, deltas = consts
    w_in = inp["w_in"][l]

    def acol(g, t):
        return ((g * 3 + t) * 8 + hd) * 128

    cols = []
    for g in range(3):
        cols += [acol(g, 0), acol(g, 1)]
    cols += [SPL[3] + hd * 128, SPL[3] + 1024 + hd * 128]
    cols += [acol(0, 2), acol(1, 2), acol(2, 2), SPL[3] + 2048 + hd * 128]
    cols += [SPL[0] + hd * 128, SPL[2] + hd * 128, SPL[4] + hd * 128]
    W = np.concatenate([w_in[:, c:c + 128] for c in cols], 1)
    WB = np.concatenate([w_in[:, SPL[1] + j * 1024 + hd * 128:SPL[1] + j * 1024 + hd * 128 + 128] for j in range(3)], 1)
    convw = np.concatenate([inp["hy_conv"][l][:, j * 1024 + hd * 128:j * 1024 + hd * 128 + 128] for j in range(3)], 1)
    rb = inp["rel_bias"]
    bias_ext = np.concatenate([rb, np.full((1, 32), NEG_MASK, np.float32)], 0)
    idx = _bma_index()
    BMA = np.zeros((128, 9, 256), np.float32)
    for g in range(3):
        BMA[:, g * 3:(g + 1) * 3, :] = bias_ext[:, g * 8 + hd][idx[:, g * 3:(g + 1) * 3, :]]
    kp = np.arange(128)[:, None, None]
    dt = (8 - np.arange(17))[None, :, None]
    qf = np.arange(128)[None, None, :]
    BC = rb[:, 24 + hd][_t5_bucket_np(128 * dt + kp - qf)].astype(np.float32)
    lam_init = 0.8 - 0.6 * math.exp(-0.3 * l)
    row = np.concatenate([[rb[15, 24 + hd], rb[31, 24 + hd], np.float32(lam_init), np.float32(1.0 - lam_init)],
                          inp["diff_lam"][l].ravel(), inp["diff_g"][l]]).astype(np.float32)
    cpar = np.tile(row[None, :], (128, 1))
    hw3 = inp["hy_w3"][l].reshape(64, 2, 2, 1024)[:, :, :, hd * 128:(hd + 1) * 128].reshape(64, 512)
    hyp = np.stack([inp["hy_b1"][l], inp["hy_freq"][l][0], inp["hy_b2"][l], inp["hy_freq"][l][1]], 1)
    hsk = np.stack([inp["hy_skip"][l][0, hd * 128:(hd + 1) * 128], inp["hy_skip"][l][1, hd * 128:(hd + 1) * 128],
                    -deltas[hd * 128:(hd + 1) * 128]], 1)
    c = np.ascontiguousarray
    Jrev = np.eye(128, dtype=np.float32)[::-1].astype(ml_dtypes.bfloat16)
    return {"W": c(W), "WB": c(WB), "convw": c(convw), "BMA": BMA, "BC": c(BC), "cpar": cpar, "Jrev": c(Jrev),
            "Zx": Zx, "tpos": tpos, "hw1": c(inp["hy_w1"][l]), "hw2": c(inp["hy_w2"][l]), "hw3": c(hw3),
            "hyp": c(hyp.astype(np.float32)), "hsk": c(hsk.astype(np.float32))}


TWO_PI = 2.0 * math.pi


def stage_B(k, ar, pp, S, d):
    Bd, RBd = d["Bd"]
    TM2d, RT2 = d["TM2d"]
    Zx, RZx = d["Zx"]
    tpos, Rtp = d["tpos"]
    hw1, Rh1 = d["hw1"]
    hw2, Rh2 = d["hw2"]
    hw3, Rh3 = d["hw3"]
    hyp, Rhyp = d["hyp"]
    hsk, Rhsk = d["hsk"]
    hx, Rhx = d["hx"]
    brB, RbrB = d["brB"]
    NA = S // 128
    CH = 2048
    ar.reset()
    w1, Rw1 = ar.alloc("w1", [64], F32)
    w2, Rw2 = ar.alloc("w2", [64], F32)
    w3, Rw3 = ar.alloc("w3", [512], F32)
    hp, Rhp = ar.alloc("hp", [4], F32)
    sk, Rsk = ar.alloc("sk", [3], F32)
    k.dma("sp", w1[0:33, :], hw1, reads=[Rh1], writes=[Rw1])
    k.dma("sp", w2[0:64, :], hw2, reads=[Rh2], writes=[Rw2])
    k.dma("sp", w3[0:64, :], hw3, reads=[Rh3], writes=[Rw3])
    k.dma("sp", hp[0:64, :], hyp, reads=[Rhyp], writes=[Rhp])
    k.dma("sp", sk, hsk, reads=[Rhsk], writes=[Rsk])
    npi, Rnpi = ar.alloc("npi", [1], F32)
    k.memset("pool", npi, 0.5 * math.pi, writes=[Rnpi])
    hq, Rhq = ar.alloc("hq", [8], F32)
    for li in range(2):
        bcol, fcol = 2 * li, 2 * li + 1
        k.act(hq[0:64, 4 * li:4 * li + 1], hp[0:64, fcol:fcol + 1], AF.Copy, reads=[Rhp], writes=[Rhq], scale=0.5)
        k.tt("dve", hq[0:64, 4 * li + 1:4 * li + 2], hq[0:64, 4 * li:4 * li + 1], hp[0:64, bcol:bcol + 1], ALU.mult,
             reads=[Rhq, Rhp], writes=[Rhq])
        k.act(hq[0:64, 4 * li + 2:4 * li + 3], hp[0:64, fcol:fcol + 1], AF.Copy, reads=[Rhp], writes=[Rhq])
        k.tt("dve", hq[0:64, 4 * li + 3:4 * li + 4], hp[0:64, fcol:fcol + 1], hp[0:64, bcol:bcol + 1], ALU.mult,
             reads=[Rhp], writes=[Rhq])
    sA, RsA = ar.alloc("sA", [CH], F32)
    sB, RsB = ar.alloc("sB", [CH], F32)
    zc = [ar.alloc(f"zc{i}", [CH], F32) for i in range(2)]
    tpc = [ar.alloc(f"tpc{i}", [CH], F32) for i in range(2)]
    h1, Rh1s = ar.alloc("h1s", [CH], F32)
    h2, Rh2s = ar.alloc("h2s", [CH], F32)
    dec, Rdec = ar.alloc("dec", [CH], F32)
    hf, Rhf = ar.alloc("hf", [CH], F32)
    hb = [ar.alloc(f"hb{i}", [CH], BF16) for i in range(4)]
    nch = 2 * S // CH
    ctr = S - 1
    hbi = 0
    for ci in range(nch):
        z, Rz = zc[ci % 2]
        tp, Rtpc = tpc[ci % 2]
        c0 = ci * CH
        k.dma("sp", z[0:33, :], Zx[:, c0:c0 + CH], reads=[RZx], writes=[Rz])
        k.dma("sp", tp, tpos[:, c0:c0 + CH].partition_broadcast(128), reads=[Rtp], writes=[Rtpc])
        for (src, Rsrc, kk, wt, Rwt, li, dst, Rdst) in ((z, Rz, 33, w1, Rw1, 0, h1, Rh1s), (h1, Rh1s, 64, w2, Rw2, 1, h2, Rh2s)):
            for sc in range(CH // 512):
                ps, Rps = pp.get()
                cs = slice(sc * 512, (sc + 1) * 512)
                k.mm(ps[0:64, :], wt[0:kk, 0:64], src[0:kk, cs], True, True, reads=[Rwt, Rsrc], writes=[Rps])
                k.act(sA[0:64, cs], ps[0:64, :], AF.Sin, reads=[Rps, Rhq], writes=[RsA],
                      bias=hq[0:64, 4 * li + 1:4 * li + 2], scale=hq[0:64, 4 * li:4 * li + 1])
                k.act(sB[0:64, cs], ps[0:64, :], AF.Abs, reads=[Rps, Rhq], writes=[RsB],
                      bias=hq[0:64, 4 * li + 3:4 * li + 4], scale=hq[0:64, 4 * li + 2:4 * li + 3])
            k.act(sB[0:64, :], sB[0:64, :], AF.Sin, reads=[RsB, Rnpi], writes=[RsB], bias=npi[0:64, :], scale=-0.5)
            k.stt("dve", dst[0:64, :], sA[0:64, :], 2.0, sB[0:64, :], ALU.mult, ALU.mult, reads=[RsA, RsB], writes=[Rdst])
        k.act(dec, tp, AF.Exp, reads=[Rtpc, Rsk], writes=[Rdec], scale=sk[:, 2:3])
        for o in range(2):
            hbt, Rhb = hb[hbi % 4]
            hbi += 1
            for sc in range(CH // 512):
                ps, Rps = pp.get()
                lo = c0 + sc * 512
                cs = slice(sc * 512, (sc + 1) * 512)
                wf = w3[0:64, (o * 2 + 0) * 128:(o * 2 + 1) * 128]
                wb = w3[0:64, (o * 2 + 1) * 128:(o * 2 + 2) * 128]
                if lo + 512 <= ctr:
                    k.mm(ps[:, :], wb, h2[0:64, cs], True, True, reads=[Rw3, Rh2s], writes=[Rps])
                elif lo > ctr:
                    k.mm(ps[:, :], wf, h2[0:64, cs], True, True, reads=[Rw3, Rh2s], writes=[Rps])
                else:
                    assert lo + 511 == ctr
                    k.mm(ps[:, :], wb, h2[0:64, cs], True, False, reads=[Rw3, Rh2s], writes=[Rps])
                    k.mm(ps[:, 511:512], wf, h2[0:64, sc * 512 + 511:sc * 512 + 512], False, True,
                         reads=[Rw3, Rh2s], writes=[Rps])
                k.tt("dve", hf[:, cs], ps[:, :], dec[:, cs], ALU.mult, reads=[Rps, Rdec], writes=[Rhf])
                if lo <= ctr < lo + 512:
                    cc = sc * 512 + (ctr - lo)
                    k.tt("dve", hf[:, cc:cc + 1], hf[:, cc:cc + 1], sk[:, o:o + 1], ALU.add,
                         reads=[Rhf, Rsk], writes=[Rhf])
            k.copy("act", hbt, hf, reads=[Rhf], writes=[Rhb])
            k.dma("sp", hx[o, :, c0:c0 + CH], hbt, reads=[Rhb], writes=[Rhx], owner=Rhb)
    k.barrier()
    ar.reset()
    GW = 2 * S - 128
    G = [ar.alloc(f"G{i}", [GW], BF16) for i in range(2)]
    ZT, RZT = ar.alloc("ZT", [NA, 128], BF16)
    ZR, RZR = ar.alloc("ZR", [NA, 128], BF16)
    Jr, RJr = ar.alloc("Jr", [128], BF16)
    Jd, RJd = d["Jrev"]
    k.dma("sp", Jr, Jd, reads=[RJd], writes=[RJr])
    ZTf = ZT.rearrange("p a c -> p (a c)")
    ZRf = ZR.rearrange("p a c -> p (a c)")

    def reverse():
        for ch in range(NA * 128 // 512):
            psr, Rpsr = pp.get()
            k.mm(psr[:, :], Jr, ZTf[:, ch * 512:(ch + 1) * 512], True, True, reads=[RJr, RZT], writes=[Rpsr])
            k.copy("act" if ch % 2 == 0 else "dve", ZRf[:, ch * 512:(ch + 1) * 512], psr[:, :], reads=[Rpsr], writes=[RZR])

    XF, RXF = ar.alloc("XF", [NA, 128], F32)
    YT, RYT = ar.alloc("YT", [NA, 128], F32)
    Bv = Bd.rearrange("(a p) c -> p a c", p=128)
    k.dma("sp", XF, Bv[:, :, 0:128], reads=[RBd], writes=[RXF])
    k.copy("dve", ZT, XF, reads=[RXF], writes=[RZT])
    hxt = hx.tensor
    for o in range(2):
        reverse()
        for c in range(128):
            g, Rg = G[c % 2]
            src = bass.AP(hxt, (o * 128 + c) * 2 * S, [[1, 128], [1, GW]])
            k.dma("sp", g, src, reads=[Rhx], writes=[Rg])
            ps, Rps = pp.get()
            order = [0] + [dd for dd in range(-(NA - 1), NA) if dd != 0]
            for ii, dd in enumerate(order):
                xd = 128 * dd + S - 128
                a_lo, a_hi = max(0, -dd), min(NA, NA - dd)
                k.mm(ps[:, a_lo + dd:a_hi + dd], g[:, xd:xd + 128], ZR[:, a_lo:a_hi, c], ii == 0, ii == len(order) - 1,
                     reads=[Rg, RZR], writes=[Rps])
            k.copy("act" if c % 2 == 0 else "dve", YT[:, :, c], ps[:, 0:NA], reads=[Rps], writes=[RYT])
        k.dma("sp", XF, Bv[:, :, 128 * (o + 1):128 * (o + 2)], reads=[RBd], writes=[RXF])
        k.tt("dve", YT, YT, XF, ALU.mult, reads=[RYT, RXF], writes=[RYT])
        if o == 0:
            k.copy("dve", ZT, YT, reads=[RYT], writes=[RZT])
    k.dma("sp", XF, TM2d[:, 128:256].rearrange("(a p) c -> p a c", p=128), reads=[RT2], writes=[RXF])
    k.act(XF, XF, AF.Silu, reads=[RXF], writes=[RXF])
    k.tt("dve", ZT, YT, XF, ALU.mult, reads=[RYT, RXF], writes=[RZT])
    k.dma("sp", brB.rearrange("(a p) c -> p a c", p=128), ZT, reads=[RZT], writes=[RbrB], owner=RZT)
    k.barrier()


def stage_C(k, ar, pp, S, d):
    FMd, RFM = d["FMd"]
    TM1d, RT1 = d["TM1d"]
    TM2d, RT2 = d["TM2d"]
    BCd, RBCd = d["BC"]
    cpard, Rcpd = d["cpar"]
    brC, RbrC = d["brC"]
    NK = S // 128
    NG = S // 512
    ar.reset()
    CQ, RCQ = ar.alloc("CQ", [S], BF16)
    CK, RCK = ar.alloc("CK", [S], BF16)
    CV, RCV = ar.alloc("CV", [NK, 130], BF16)
    BC, RBC = ar.alloc("BC", [17 * 128], F32)
    cp, Rcp = ar.alloc("cpar", [388], F32)
    k.dma("sp", CQ, FMd[6, :, :], reads=[RFM], writes=[RCQ])
    k.dma("sp", CK, FMd[7, :, :], reads=[RFM], writes=[RCK])
    k.memset("pool", CV[:, :, 128:130], 1.0, writes=[RCV])
    k.dma("sp", CV[:, :, 0:128], TM1d[:, 384:512].rearrange("(t p) c -> p t c", p=128), reads=[RT1], writes=[RCV])
    k.dma("sp", BC, BCd.rearrange("p j q -> p (j q)"), reads=[RBCd], writes=[RBC])
    k.dma("sp", cp, cpard, reads=[Rcpd], writes=[Rcp])
    sm, Rsm = ar.alloc("smallC", [16], F32)
    t64, Rt64 = ar.alloc("t64", [2, 64], F32)
    k.tt("dve", t64[:, 0, :], cp[:, 4:68], cp[:, 68:132], ALU.mult, reads=[Rcp], writes=[Rt64])
    k.tt("dve", t64[:, 1, :], cp[:, 132:196], cp[:, 196:260], ALU.mult, reads=[Rcp], writes=[Rt64])
    k.op("dve", lambda e: e.reduce_sum(sm[:, 0:2], t64, AX.X), reads=[Rt64], writes=[Rsm])
    k.act(sm[:, 2:4], sm[:, 0:2], AF.Exp, reads=[Rsm], writes=[Rsm])
    k.tt("dve", sm[:, 4:5], sm[:, 2:3], sm[:, 3:4], ALU.subtract, reads=[Rsm], writes=[Rsm])
    k.tt("dve", sm[:, 4:5], sm[:, 4:5], cp[:, 2:3], ALU.add, reads=[Rsm, Rcp], writes=[Rsm])
    k.act(sm[:, 5:6], sm[:, 4:5], AF.Copy, reads=[Rsm], writes=[Rsm], scale=-1.0)
    k.memset("pool", sm[:, 6:7], 1e-6, writes=[Rsm])
    dgs, Rdgs = ar.alloc("dgs", [128], F32)
    k.act(dgs, cp[:, 260:388], AF.Copy, reads=[Rcp], writes=[Rdgs], scale=cp[:, 3:4])
    tmpb = [ar.alloc(f"tmpC{i}", [512], F32) for i in range(2)]
    PT = [ar.alloc(f"PTC{i}", [512], BF16) for i in range(4)]
    gt = [ar.alloc(f"gtC{i}", [4, 128], F32) for i in range(2)]
    ob = [ar.alloc(f"obC{i}", [4, 128], BF16) for i in range(2)]
    t2, Rt2 = ar.alloc("t2C", [128], F32)
    cpre, Rcpre = ar.alloc("cpre", [128], F32)
    sq, Rsq = ar.alloc("sqC", [128], F32)
    ep, Rep = ar.alloc("epC", [8], F32)
    accR = [Res(f"accC{a}") for a in range(8)]

    def acc_ap(a):
        return pp.t[5 + a // 3][:, (a % 3) * 160:(a % 3) * 160 + 130]

    Gv = TM2d[:, 256:384].rearrange("(t p) c -> p t c", p=128)
    Ov = brC.rearrange("(t p) c -> p t c", p=128)
    pp.n = 5
    pp.i = 0
    it = 0
    for Gq in range(NG):
        qt0 = 4 * Gq
        g, Rg = gt[Gq % 2]
        o, Ro = ob[Gq % 2]
        k.dma("sp", g, Gv[:, qt0:qt0 + 4, :], reads=[RT2], writes=[Rg])
        k.act(g, g, AF.Silu, reads=[Rg], writes=[Rg])
        for kt in range(NK):
            near = (qt0 - 5 <= kt <= qt0 + 8)
            for c in range(2):
                ps, Rps = pp.get()
                k.mm(ps[:, :], CK[64 * c:64 * c + 64, kt * 128:(kt + 1) * 128], CQ[64 * c:64 * c + 64, Gq * 512:(Gq + 1) * 512],
                     True, True, reads=[RCK, RCQ], writes=[Rps])
                pt, Rpt = PT[it % 4]
                if near:
                    tb, Rtb = tmpb[it % 2]
                    j0 = 8 - (kt - qt0)
                    k.stt("dve", tb, ps[:, :], SCALE_C, BC[:, j0 * 128:(j0 + 4) * 128], ALU.mult, ALU.add,
                          reads=[Rps, RBC], writes=[Rtb])
                    k.act(pt, tb, AF.Exp, reads=[Rtb], writes=[Rpt])
                else:
                    bcol = cp[:, 0:1] if kt < qt0 else cp[:, 1:2]
                    k.act(pt, ps[:, :], AF.Exp, reads=[Rps, Rcp], writes=[Rpt], bias=bcol, scale=SCALE_C)
                it += 1
                for qi in range(4):
                    a = c * 4 + qi
                    k.mm(acc_ap(a), pt[:, qi * 128:(qi + 1) * 128], CV[:, kt, :], kt == 0 and a % 3 == 0, kt == NK - 1,
                         reads=[Rpt, RCV], writes=[accR[a]])
        for qi in range(4):
            O1, O2 = acc_ap(qi), acc_ap(4 + qi)
            R1, R2 = accR[qi], accR[4 + qi]
            k.op("dve", lambda e, O1=O1: e.reciprocal(ep[:, 0:1], O1[:, 128:129]), reads=[R1], writes=[Rep])
            k.op("dve", lambda e, O2=O2: e.reciprocal(ep[:, 1:2], O2[:, 128:129]), reads=[R2], writes=[Rep])
            k.tt("dve", ep[:, 1:2], ep[:, 1:2], sm[:, 5:6], ALU.mult, reads=[Rep, Rsm], writes=[Rep])
            k.act(t2, O2[:, 0:128], AF.Copy, reads=[R2, Rep], writes=[Rt2], scale=ep[:, 1:2])
            k.stt("dve", cpre, O1[:, 0:128], ep[:, 0:1], t2, ALU.mult, ALU.add, reads=[R1, Rep, Rt2], writes=[Rcpre])
            k.tt("dve", sq, cpre, cpre, ALU.mult, reads=[Rcpre], writes=[Rsq])
            k.op("dve", lambda e: e.reduce_sum(ep[:, 2:3], sq, AX.X), reads=[Rsq], writes=[Rep])
            k.act(ep[:, 3:4], ep[:, 2:3], AF.Sqrt, reads=[Rep, Rsm], writes=[Rep], bias=sm[:, 6:7], scale=1.0 / 128.0)
            k.op("dve", lambda e: e.reciprocal(ep[:, 4:5], ep[:, 3:4]), reads=[Rep], writes=[Rep])
            k.stt("dve", cpre, cpre, ep[:, 4:5], dgs, ALU.mult, ALU.mult, reads=[Rcpre, Rep, Rdgs], writes=[Rcpre])
            k.tt("dve", o[:, qi, :], cpre, g[:, qi, :], ALU.mult, reads=[Rcpre, Rg], writes=[Ro])
        k.dma("sp", Ov[:, qt0:qt0 + 4, :], o, reads=[Ro], writes=[RbrC], owner=Ro)
    pp.n = 8
    pp.i = 0
    k.barrier()


def _rmsnorm_tile(k, xt, Rx, gt, Rgt, sq, Rsq, ep, Rep, out, Rout, D):
    k.tt("dve", sq, xt, xt, ALU.mult, reads=[Rx], writes=[Rsq])
    k.op("dve", lambda e: e.reduce_sum(ep[:, 0:1], sq, AX.X), reads=[Rsq], writes=[Rep])
    k.act(ep[:, 1:2], ep[:, 0:1], AF.Sqrt, reads=[Rep], writes=[Rep], bias=ep[:, 3:4], scale=1.0 / D)
    k.op("dve", lambda e: e.reciprocal(ep[:, 2:3], ep[:, 1:2]), reads=[Rep], writes=[Rep])
    k.stt("dve", out, xt, ep[:, 2:3], gt, ALU.mult, ALU.mult, reads=[Rx, Rep, Rgt], writes=[Rout])


def build_L0(T):
    nc = bass.Bass("TRN2", target_bir_lowering=False)
    k = K(nc)
    ar = Arena(nc)
    x = nc.dram_tensor("x", [T, 2048], F32, kind="ExternalInput").ap()
    g = nc.dram_tensor("g", [1, 2048], F32, kind="ExternalInput").ap()
    hn = nc.dram_tensor("hn", [T, 2048], BF16, kind="ExternalOutput").ap()
    Rx, Rg, Rhn = DRes("x"), DRes("g"), DRes("hn")
    gt, Rgt = ar.alloc("gt", [2048], F32)
    k.dma("sp", gt, g.partition_broadcast(128), reads=[Rg], writes=[Rgt])
    xt = [ar.alloc(f"xt{i}", [2048], F32) for i in range(2)]
    ot = [ar.alloc(f"ot{i}", [2048], BF16) for i in range(2)]
    sq, Rsq = ar.alloc("sq", [2048], F32)
    ep, Rep = ar.alloc("ep", [4], F32)
    k.memset("pool", ep[:, 3:4], 1e-6, writes=[Rep])
    for t in range(T // 128):
        xx, Rxx = xt[t % 2]
        oo, Roo = ot[t % 2]
        k.dma("sp", xx, x[t * 128:(t + 1) * 128, :], reads=[Rx], writes=[Rxx])
        _rmsnorm_tile(k, xx, Rxx, gt, Rgt, sq, Rsq, ep, Rep, oo, Roo, 2048.0)
        k.dma("sp", hn[t * 128:(t + 1) * 128, :], oo, reads=[Roo], writes=[Rhn], owner=Roo)
    k.emit()
    return nc


def build_L23(TT, last, T=1024):
    nc = bass.Bass("TRN2", target_bir_lowering=False)
    k = K(nc)
    ar = Arena(nc)
    pp = PsumPool(nc)
    ODT = F32 if last else BF16
    x = nc.dram_tensor("x", [TT, 2048], F32, kind="ExternalInput").ap()
    hT = nc.dram_tensor("hT", [2048, TT], BF16, kind="ExternalInput").ap()
    brT = nc.dram_tensor("brT", [3072, TT], BF16, kind="ExternalInput").ap()
    Wm = nc.dram_tensor("Wm", [2048, 6144], F32, kind="ExternalInput").ap()
    mbT = nc.dram_tensor("mbT", [128, 48], F32, kind="ExternalInput").ap()
    Wp = nc.dram_tensor("Wp", [3072, 2048], F32, kind="ExternalInput").ap()
    Wo = nc.dram_tensor("Wo", [2048, 2048], F32, kind="ExternalInput").ap()
    g = nc.dram_tensor("g", [1, 2048], F32, kind="ExternalInput").ap()
    xn = nc.dram_tensor("xn", [TT, 2048], F32, kind="ExternalOutput").ap()
    hn = nc.dram_tensor("hn", [TT, 2048], ODT, kind="ExternalOutput").ap()
    RD = {n: DRes(n) for n in ("x", "hT", "brT", "Wm", "mbT", "Wp", "Wo", "g", "xn", "hn")}
    NTG = T // 512
    Wmv = Wm.rearrange("(kc p) (n d) -> p kc n d", p=128, n=3)
    Wpv = Wp.rearrange("(kc p) d -> p kc d", p=128)
    Wov = Wo.rearrange("(kc p) e -> p kc e", p=128)
    hTv = hT.rearrange("(kc p) t -> p kc t", p=128)
    brTv = brT.rearrange("(kc p) t -> p kc t", p=128)
    hs, Rhs = ar.alloc("hs", [16, T], BF16)
    bs, Rbs = ar.alloc("bs", [24, T], BF16)
    yT, RyT = ar.alloc("yT", [16, T], BF16)
    p1_end = ar.off
    mb, Rmb = ar.alloc("mb", [48], F32)
    wm = [ar.alloc(f"wm{i}", [16, 3, 128], BF16) for i in range(2)]
    wp = [ar.alloc(f"wp{i}", [24, 128], BF16) for i in range(2)]
    gate = [ar.alloc(f"gate{i}", [512], F32) for i in range(2)]
    yacc, Ryacc = ar.alloc("yacc", [512], F32)
    tmpy, Rtmpy = ar.alloc("tmpy", [512], F32)
    RyTs = [Res(f"yT{dc}") for dc in range(16)]
    ar.off = 0
    wo, _ = ar.alloc("wo", [16, 2048], BF16)
    assert ar.off <= (16 + 24) * T * 2
    ar.off = p1_end
    Rwoq = [Res(f"wo{q}") for q in range(4)]
    gt, Rgt = ar.alloc("gt", [2048], F32)
    xt = [ar.alloc(f"xt{i}", [2048], F32) for i in range(2)]
    ot = [ar.alloc(f"ot{i}", [2048], ODT) for i in range(2)]
    sq, Rsq = ar.alloc("sq", [2048], F32)
    ep, Rep = ar.alloc("ep", [4], F32)
    it = 0
    for blk in range(TT // T):
        tb = slice(blk * T, (blk + 1) * T)
        k.dma("sp", hs, hTv[:, :, tb], reads=[RD["hT"]], writes=[Rhs])
        k.dma("sp", bs, brTv[:, :, tb], reads=[RD["brT"]], writes=[Rbs])
        k.dma("sp", mb, mbT, reads=[RD["mbT"]], writes=[Rmb])
        for dc in range(16):
            wmt, Rwm = wm[dc % 2]
            wpt, Rwp = wp[dc % 2]
            for n in range(3):
                k.dma("pool", wmt[:, :, n, :], Wmv[:, :, n, dc * 128:(dc + 1) * 128], reads=[RD["Wm"]], writes=[Rwm])
            k.dma("pool", wpt, Wpv[:, :, dc * 128:(dc + 1) * 128], reads=[RD["Wp"]], writes=[Rwp])
            for tg in range(NTG):
                ts_ = slice(tg * 512, (tg + 1) * 512)
                for n in range(3):
                    psm, Rpsm = pp.get()
                    for kc in range(16):
                        k.mm(psm[:, :], wmt[:, kc, n, :], hs[:, kc, ts_], kc == 0, kc == 15, reads=[Rwm, Rhs], writes=[Rpsm])
                    psp, Rpsp = pp.get()
                    for wc in range(8):
                        k.mm(psp[:, :], wpt[:, n * 8 + wc, :], bs[:, n * 8 + wc, ts_], wc == 0, wc == 7,
                             reads=[Rwp, Rbs], writes=[Rpsp])
                    gt_, Rgt_ = gate[it % 2]
                    it += 1
                    k.act(gt_, psm[:, :], AF.Sigmoid, reads=[Rpsm, Rmb], writes=[Rgt_], bias=mb[:, n * 16 + dc:n * 16 + dc + 1])
                    if n == 0:
                        k.tt("dve", yacc, gt_, psp[:, :], ALU.mult, reads=[Rgt_, Rpsp], writes=[Ryacc])
                    else:
                        k.tt("dve", tmpy, gt_, psp[:, :], ALU.mult, reads=[Rgt_, Rpsp], writes=[Rtmpy])
                        if n == 1:
                            k.tt("dve", yacc, yacc, tmpy, ALU.add, reads=[Ryacc, Rtmpy], writes=[Ryacc])
                        else:
                            k.tt("dve", yT[:, dc, ts_], yacc, tmpy, ALU.add, reads=[Ryacc, Rtmpy], writes=[RyTs[dc]])
        k.barrier()
        for q in range(4):
            k.dma("pool", wo[:, 4 * q:4 * q + 4, :], Wov[:, 4 * q:4 * q + 4, :], reads=[RD["Wo"]], writes=[Rwoq[q]])
        k.dma("sp", gt, g.partition_broadcast(128), reads=[RD["g"]], writes=[Rgt])
        k.memset("pool", ep[:, 3:4], 1e-6, writes=[Rep])
        for t in range(T // 128):
            xx, Rxx = xt[t % 2]
            oo, Roo = ot[t % 2]
            r0 = blk * T + t * 128
            k.dma("sp", xx, x[r0:r0 + 128, :], reads=[RD["x"]], writes=[Rxx])
            for ec in range(4):
                ps, Rps = pp.get()
                for kc in range(16):
                    k.mm(ps[:, :], yT[:, kc, t * 128:(t + 1) * 128], wo[:, kc, ec * 512:(ec + 1) * 512], kc == 0, kc == 15,
                         reads=[RyTs[kc], Rwoq[kc // 4]], writes=[Rps])
                k.tt("dve", xx[:, ec * 512:(ec + 1) * 512], xx[:, ec * 512:(ec + 1) * 512], ps[:, :], ALU.add,
                     reads=[Rxx, Rps], writes=[Rxx])
            k.dma("sp", xn[r0:r0 + 128, :], xx, reads=[Rxx], writes=[RD["xn"]], owner=Rxx)
            _rmsnorm_tile(k, xx, Rxx, gt, Rgt, sq, Rsq, ep, Rep, oo, Roo, 2048.0)
            k.dma("sp", hn[r0:r0 + 128, :], oo, reads=[Roo], writes=[RD["hn"]], owner=Roo)
        k.barrier()
    k.emit()
    return nc


_PROGS = {}


def _prog(name, fn):
    if name not in _PROGS:
        _PROGS[name] = fn()
    return _PROGS[name]


def kernel(x, norm_g, final_g, w_in, merge_b, rel_bias, hy_conv, hy_w1, hy_b1, hy_freq, hy_w2, hy_b2, hy_w3,
           hy_skip, diff_lam, diff_g, w_proj, w_out):
    NCORE = 8
    S, D, DEPTH = 8192, 2048, 4
    T = S // NCORE
    cores = list(range(NCORE))
    c = np.ascontiguousarray
    inp = dict(w_in=np.asarray(w_in), rel_bias=np.asarray(rel_bias), hy_conv=np.asarray(hy_conv), hy_w1=np.asarray(hy_w1),
               hy_b1=np.asarray(hy_b1), hy_freq=np.asarray(hy_freq), hy_w2=np.asarray(hy_w2), hy_b2=np.asarray(hy_b2),
               hy_w3=np.asarray(hy_w3), hy_skip=np.asarray(hy_skip), diff_lam=np.asarray(diff_lam), diff_g=np.asarray(diff_g))
    norm_g = np.asarray(norm_g, np.float32)
    final_g = np.asarray(final_g, np.float32)
    merge_b = np.asarray(merge_b, np.float32)
    w_proj = np.asarray(w_proj)
    w_out = np.asarray(w_out)
    xfull = c(np.asarray(x)[0])
    xs = [c(xfull[i * T:(i + 1) * T, :]) for i in range(NCORE)]
    consts = hyena_consts(S)
    nc0 = _prog("L0", lambda: build_L0(T))
    res = run_bass_kernel_spmd(nc0, [{"x": xs[i], "g": c(norm_g[0][None, :])} for i in range(NCORE)], core_ids=cores)
    hn = [res.results[i]["hn"] for i in range(NCORE)]
    out = None
    for l in range(DEPTH):
        hT = c(np.concatenate(hn, axis=0).T)
        nc1 = _prog("L1", lambda: build_L1(S))
        maps = []
        for hd in range(NCORE):
            m = l1_inputs(inp, l, hd, S, consts)
            m["hT"] = hT
            maps.append(m)
        res = run_bass_kernel_spmd(nc1, maps, core_ids=cores)
        br = np.stack([np.concatenate([res.results[hd][nm] for hd in range(NCORE)], axis=1) for nm in ("brA", "brB", "brC")], 0)
        del res
        last = (l == DEPTH - 1)
        nc2 = _prog("L23_last" if last else "L23", lambda: build_L23(S, last))
        gnext = final_g if last else norm_g[l + 1]
        Wm = c(inp["w_in"][l][:, SPL[5]:])
        mbT = c(merge_b[l].reshape(3, 16, 128).transpose(2, 0, 1).reshape(128, 48))
        Wp = c(w_proj[l].reshape(3072, 2048))
        Wo = c(w_out[l])
        brT = c(br.transpose(0, 2, 1).reshape(3072, S))
        m = {"x": xfull, "hT": hT, "brT": brT, "Wm": Wm, "mbT": mbT, "Wp": Wp, "Wo": Wo, "g": c(gnext[None, :])}
        res = run_bass_kernel_spmd(nc2, [m], core_ids=[0])
        xfull = res.results[0]["xn"]
        hn = [res.results[0]["hn"]]
        del res
    out = np.concatenate(hn, axis=0).astype(np.float32, copy=False)[None]
    return out
```

```python
import math
import numpy as np
import ml_dtypes
import concourse.bass as bass
import concourse.mybir as mybir
from concourse.bass_utils import run_bass_kernel_spmd

F32 = mybir.dt.float32
BF16 = mybir.dt.bfloat16
AF = mybir.ActivationFunctionType
ALU = mybir.AluOpType
AX = mybir.AxisListType

PAGE = 20000


class Res:
    __slots__ = ("name", "w", "r", "dsem", "dcnt")

    def __init__(self, name):
        self.name = name
        self.w = None
        self.r = {}
        self.dsem = None
        self.dcnt = 0


class K:
    ENG = ("pe", "act", "dve", "pool", "sp")

    def __init__(self, nc):
        self.nc = nc
        self.e = {"pe": nc.tensor, "act": nc.scalar, "dve": nc.vector, "pool": nc.gpsimd, "sp": nc.sync}
        self.sem = {}
        self.cnt = {}
        self.page = {}
        self.waited = {}
        self.nsem = 0
        self.sem_owner = {}
        self.recycle = False
        self.dpool = []
        self.dlive = []
        for n in self.ENG:
            self.cnt[n] = 0
            self.page[n] = 0
            self.sem[n] = self.new_sem(f"s_{n}_0", n)
        self.dma_events = {}
        self.all_sems = []
        self.q = {n: [] for n in self.ENG}

    def new_sem(self, name, owner=None):
        self.nsem += 1
        sem = self.nc.alloc_semaphore(f"{name}_{self.nsem}")
        if owner is not None:
            self.sem_owner[id(sem)] = owner
        return sem

    def _wait(self, eng, ev):
        if ev is None:
            return
        sem, val = ev
        if eng == "pe" and self.sem_owner.get(id(sem)) == "pe":
            return
        key = (eng, id(sem))
        if self.waited.get(key, 0) >= val:
            return
        self.waited[key] = val
        self.q[eng].append(("w", sem, val))

    def _deps(self, eng, reads, writes):
        for r in reads:
            if hasattr(r, "wall"):
                for ev in r.wall.values():
                    self._wait(eng, ev)
            else:
                self._wait(eng, r.w)
        for w in writes:
            if not hasattr(w, "wall"):
                self._wait(eng, w.w)
            for ev in w.r.values():
                self._wait(eng, ev)

    def _commit(self, ev, reads, writes):
        for r in reads:
            old = r.r.get(id(ev[0]))
            if old is None or old[1] < ev[1]:
                r.r[id(ev[0])] = ev
        for w in writes:
            if hasattr(w, "wall"):
                old = w.wall.get(id(ev[0]))
                if old is None or old[1] < ev[1]:
                    w.wall[id(ev[0])] = ev
            else:
                w.w = ev
            w.r = {}

    def op(self, eng, fn, reads=(), writes=(), sig=True):
        self._deps(eng, reads, writes)
        if sig:
            if self.cnt[eng] >= PAGE:
                self.sem[eng] = self.new_sem(f"s_{eng}", eng)
                self.cnt[eng] = 0
            self.cnt[eng] += 1
            self.q[eng].append(("i", fn, self.sem[eng], 1))
            ev = (self.sem[eng], self.cnt[eng])
        else:
            if self.cnt[eng] >= PAGE:
                self.sem[eng] = self.new_sem(f"s_{eng}", eng)
                self.cnt[eng] = 0
            self.q[eng].append(("i", fn, None, 0))
            ev = (self.sem[eng], self.cnt[eng] + 1)
        self._commit(ev, reads, writes)

    def dma(self, eng, out, in_, reads=(), writes=(), owner=None, **kw):
        self._deps(eng, reads, writes)
        if owner is None:
            owner = writes[0]
        self._dsem(owner)
        self.q[eng].append(("i", lambda e: e.dma_start(out=out, in_=in_, **kw), owner.dsem, 16))
        owner.dcnt += 16
        ev = (owner.dsem, owner.dcnt)
        self.dma_events[id(owner.dsem)] = ev
        self._commit(ev, reads, writes)

    def _dsem(self, owner):
        if owner.dsem is None:
            if self.dpool:
                owner.dsem, owner.dcnt = self.dpool.pop()
            else:
                owner.dsem = self.new_sem("d")
            self.dlive.append(owner)

    def custom(self, eng, fn, inc, reads=(), writes=(), owner=None):
        self._deps(eng, reads, writes)
        self._dsem(owner)
        self.q[eng].append(("i", fn, owner.dsem, inc))
        owner.dcnt += inc
        ev = (owner.dsem, owner.dcnt)
        self.dma_events[id(owner.dsem)] = ev
        self._commit(ev, reads, writes)

    def barrier(self, engines=None):
        engines = engines or self.ENG
        evs = [(self.sem[s], self.cnt[s]) for s in self.ENG if self.cnt[s] > 0]
        evs += list(self.dma_events.values())
        for eng in engines:
            for ev in evs:
                self._wait(eng, ev)
        self.dma_events = {}
        if self.recycle and len(engines) == len(self.ENG):
            for r in self.dlive:
                self.dpool.append((r.dsem, r.dcnt))
                r.dsem = None
            self.dlive = []

    def emit(self):
        self.barrier()
        nc = self.nc
        q = self.q

        def run(e, items):
            for it in items:
                if it[0] == "w":
                    e.wait_ge(it[1], it[2])
                else:
                    ins = it[1](e)
                    if it[2] is not None:
                        ins.then_inc(it[2], it[3])

        with nc.Block() as block:
            @block.tensor
            def _(e):
                run(e, q["pe"])

            @block.scalar
            def _(e):
                run(e, q["act"])

            @block.vector
            def _(e):
                run(e, q["dve"])

            @block.gpsimd
            def _(e):
                run(e, q["pool"])

            @block.sync
            def _(e):
                run(e, q["sp"])

    def mm(self, out, lhsT, rhs, start, stop, reads, writes):
        self.op("pe", lambda e: e.matmul(out, lhsT, rhs, start=start, stop=stop), reads, writes, sig=stop)

    def act(self, out, in_, func, reads, writes, bias=None, scale=None):
        kw = {}
        if bias is not None:
            kw["bias"] = bias
        if scale is not None:
            kw["scale"] = scale
        self.op("act", lambda e: e.activation(out=out, in_=in_, func=func, **kw), reads, writes)

    def copy(self, eng, out, in_, reads, writes):
        if eng == "act":
            self.op("act", lambda e: e.copy(out, in_), reads, writes)
        else:
            self.op(eng, lambda e: e.tensor_copy(out, in_), reads, writes)

    def tt(self, eng, out, in0, in1, op, reads, writes):
        self.op(eng, lambda e: e.tensor_tensor(out, in0, in1, op), reads, writes)

    def ts(self, eng, out, in0, s1, s2, op0, op1, reads, writes):
        if s2 is None:
            self.op(eng, lambda e: e.tensor_scalar(out, in0, s1, None, op0), reads, writes)
        else:
            self.op(eng, lambda e: e.tensor_scalar(out, in0, s1, s2, op0, op1), reads, writes)

    def stt(self, eng, out, in0, scalar, in1, op0, op1, reads, writes):
        self.op(eng, lambda e: e.scalar_tensor_tensor(out, in0, scalar, in1, op0, op1), reads, writes)

    def memset(self, eng, ap, val, writes):
        self.op(eng, lambda e: e.memset(ap, val), (), writes)


class PsumPool:
    def __init__(self, nc, n=8):
        self.t = [nc.alloc_psum_tensor(f"psb{i}", [128, 512], F32) for i in range(n)]
        self.r = [Res(f"psb{i}") for i in range(n)]
        self.i = 0
        self.n = n

    def get(self):
        i = self.i
        self.i = (self.i + 1) % self.n
        return self.t[i], self.r[i]


class DRes(Res):
    __slots__ = ("wall",)

    def __init__(self, name):
        super().__init__(name)
        self.wall = {}


def _prod(x):
    r = 1
    for v in x:
        r *= v
    return r


class Arena:
    def __init__(self, nc, nbytes=204800):
        self.cap = nbytes
        self.t = nc.alloc_sbuf_tensor("arena", [128, nbytes // 4], F32)
        self.off = 0
        self.n = 0

    def reset(self):
        self.off = 0

    def alloc(self, name, free_shape, dtype):
        esz = 2 if dtype == BF16 else 4
        n = _prod(free_shape)
        nb = (n * esz + 31) // 32 * 32
        assert self.off + nb <= self.cap, f"arena overflow at {name}: {self.off + nb}"
        a = self.t[:, self.off // 4:(self.off + nb) // 4]
        self.off += nb
        if dtype != F32:
            a = a.bitcast(dtype)
        a = a[:, 0:n]
        if len(free_shape) == 2:
            a = a.rearrange("p (a b) -> p a b", a=free_shape[0])
        elif len(free_shape) == 3:
            a = a.rearrange("p (a b c) -> p a b c", a=free_shape[0], b=free_shape[1])
        self.n += 1
        return a, Res(f"{name}_{self.n}")


SCALE_A = 1.0 / math.sqrt(128.0)
SCALE_C = 1.0 / math.sqrt(64.0)
NEG_MASK = -30000.0


def stage_P(k, ar, pp, S, d):
    ar.reset()
    NT = S // 512
    hT, RhTd = d["hT"]
    W, RW = d["W"]
    WB, RWB = d["WB"]
    convw, Rcv = d["convw"]
    FMd, RFM = d["FMd"]
    TM1d, RT1 = d["TM1d"]
    TM2d, RT2 = d["TM2d"]
    Bd, RBd = d["Bd"]
    wsb, _ = ar.alloc("wsb", [16, 1920], BF16)
    Rw = [Res(f"wsb{q}") for q in range(4)]
    wB3, _ = ar.alloc("wB3", [3, 16, 384], BF16)
    RwB = [Res(f"wB3{q}") for q in range(4)]
    wBf, RwBf = ar.alloc("wBf", [4, 384], F32)
    cw, Rcw = ar.alloc("cw", [3, 384], F32)
    hTt = [ar.alloc(f"hTt{i}", [16, 514], BF16) for i in range(2)]
    stF = [ar.alloc(f"stF{i}", [8, 512], BF16)[0] for i in range(2)]
    RstF = [[Res(f"stF{i}_{b}") for b in range(8)] for i in range(2)]
    stT1 = [ar.alloc(f"stT1{i}", [4, 512], BF16)[0] for i in range(2)]
    RstT1 = [[Res(f"stT1{i}_{b}") for b in range(4)] for i in range(2)]
    stT2 = [ar.alloc(f"stT2{i}", [4, 384], F32)[0] for i in range(2)]
    RstT2 = [[Res(f"stT2{i}_{b}") for b in range(4)] for i in range(2)]
    stB = [ar.alloc(f"stB{i}", [4, 384], F32)[0] for i in range(2)]
    RstB = [[Res(f"stB{i}_{b}") for b in range(4)] for i in range(2)]

    Wv = W.rearrange("(kc p) c -> p kc c", p=128)
    WBv = WB.rearrange("(kc p) c -> p kc c", p=128)
    hTv = hT.rearrange("(kc p) s -> p kc s", p=128)
    for q in range(4):
        k.dma("pool", wsb[:, 4 * q:4 * q + 4, :], Wv[:, 4 * q:4 * q + 4, :], reads=[RW], writes=[Rw[q]])
    k.dma("sp", cw, convw.partition_broadcast(128), reads=[Rcv], writes=[Rcw])
    for q in range(4):
        k.dma("sp", wBf, WBv[:, 4 * q:4 * q + 4, :], reads=[RWB], writes=[RwBf])
        for ksh in range(3):
            k.tt("dve", wB3[:, ksh, 4 * q:4 * q + 4, :], wBf, cw[:, ksh:ksh + 1, :].to_broadcast([128, 4, 384]),
                 ALU.mult, reads=[RwBf, Rcw], writes=[RwB[q]])

    def load(i):
        b = i % 2
        t, r = hTt[b]
        c0 = i * 512 - 1
        c1 = i * 512 + 513
        lo, hi = 0, 514
        if i == 0:
            k.memset("pool", t[:, :, 0:1], 0.0, writes=[r])
            c0, lo = 0, 1
        if i == NT - 1:
            k.memset("pool", t[:, :, 513:514], 0.0, writes=[r])
            c1, hi = S, 513
        k.dma("sp", t[:, :, lo:hi], hTv[:, :, c0:c1], reads=[RhTd], writes=[r])

    load(0)
    ev = 0
    for i in range(NT):
        b = i % 2
        if i + 1 < NT:
            load(i + 1)
        t, r = hTt[b]
        for blk in range(8):
            ps, Rps = pp.get()
            for kc in range(16):
                k.mm(ps[:, :], wsb[:, kc, blk * 128:(blk + 1) * 128], t[:, kc, 1:513], kc == 0, kc == 15,
                     reads=[Rw[kc // 4], r], writes=[Rps])
            k.copy("act" if ev % 2 == 0 else "dve", stF[b][:, blk, :], ps[:, :], reads=[Rps], writes=[RstF[b][blk]])
            ev += 1
        k.dma("sp", FMd[:, :, i * 512:(i + 1) * 512].rearrange("b p s -> p b s"), stF[b],
              reads=RstF[b], writes=[RFM], owner=RstF[b][0])
        for sub in range(4):
            ps, Rps = pp.get()
            for kc in range(16):
                k.mm(ps[:, :], t[:, kc, 1 + sub * 128:129 + sub * 128], wsb[:, kc, 1024:1536], kc == 0, kc == 15,
                     reads=[Rw[kc // 4], r], writes=[Rps])
            k.copy("act" if ev % 2 == 0 else "dve", stT1[b][:, sub, :], ps[:, :], reads=[Rps], writes=[RstT1[b][sub]])
            ev += 1
            ps, Rps = pp.get()
            for kc in range(16):
                k.mm(ps[:, 0:384], t[:, kc, 1 + sub * 128:129 + sub * 128], wsb[:, kc, 1536:1920], kc == 0, kc == 15,
                     reads=[Rw[kc // 4], r], writes=[Rps])
            k.copy("act" if ev % 2 == 0 else "dve", stT2[b][:, sub, :], ps[:, 0:384], reads=[Rps], writes=[RstT2[b][sub]])
            ev += 1
            ps, Rps = pp.get()
            for ksh in range(3):
                for kc in range(16):
                    k.mm(ps[:, 0:384], t[:, kc, ksh + sub * 128:ksh + sub * 128 + 128], wB3[:, ksh, kc, :],
                         ksh == 0 and kc == 0, ksh == 2 and kc == 15, reads=[RwB[kc // 4], r], writes=[Rps])
            k.copy("act" if ev % 2 == 0 else "dve", stB[b][:, sub, :], ps[:, 0:384], reads=[Rps], writes=[RstB[b][sub]])
            ev += 1
        rows = slice(i * 512, (i + 1) * 512)
        k.dma("sp", TM1d[rows, :].rearrange("(sub p) c -> p sub c", p=128), stT1[b], reads=RstT1[b], writes=[RT1], owner=RstT1[b][0])
        k.dma("sp", TM2d[rows, :].rearrange("(sub p) c -> p sub c", p=128), stT2[b], reads=RstT2[b], writes=[RT2], owner=RstT2[b][0])
        k.dma("sp", Bd[rows, :].rearrange("(sub p) c -> p sub c", p=128), stB[b], reads=RstB[b], writes=[RBd], owner=RstB[b][0])
    k.barrier()


def stage_A(k, ar, pp, S, d):
    FMd, RFM = d["FMd"]
    TM1d, RT1 = d["TM1d"]
    TM2d, RT2 = d["TM2d"]
    BMA, RBMA = d["BMA"]
    OG, ROG = d["OG"]
    brA, RbrA = d["brA"]
    ar.reset()
    bm, Rbm = ar.alloc("bm", [9, 256], F32)
    k.dma("sp", bm, BMA, reads=[RBMA], writes=[Rbm])
    QTn, RQn = ar.alloc("QTn", [S], BF16)
    KTn, RKn = ar.alloc("KTn", [S], BF16)
    QTp, RQp = ar.alloc("QTp", [S], BF16)
    KTp, RKp = ar.alloc("KTp", [S + 16 * 128], BF16)
    VP, RVP = ar.alloc("VP", [S // 128 + 16, 130], BF16)
    Ost = [ar.alloc(f"Ost{i}", [64, 130], F32) for i in range(2)]
    tmp = [ar.alloc(f"tmpA{i}", [512], F32) for i in range(2)]
    PT = [ar.alloc(f"PT{i}", [512], BF16) for i in range(2)]
    it = 0
    for g, dil in enumerate((1, 4, 16)):
        n = S // dil
        J = n // 128
        k.dma("sp", QTn, FMd[2 * g, :, :], reads=[RFM], writes=[RQn])
        k.dma("sp", KTn, FMd[2 * g + 1, :, :], reads=[RFM], writes=[RKn])
        QTpv = QTp.rearrange("p (r m) -> p r m", r=dil)
        KTpv = KTp[:, 0:dil * (n + 128)].rearrange("p (r m) -> p r m", r=dil)
        VPv = VP[:, 0:dil * (J + 1), :].rearrange("p (r j) c -> p r j c", r=dil)
        k.memset("pool", KTp, 0.0, writes=[RKp])
        k.memset("pool", VP, 0.0, writes=[RVP])
        k.memset("pool", VP[:, :, 128:129], 1.0, writes=[RVP])
        Qsrc = QTn.rearrange("p (m r) -> p r m", r=dil)
        Ksrc = KTn.rearrange("p (m r) -> p r m", r=dil)
        for r in range(dil):
            k.copy("pool" if r % 2 == 0 else "dve", QTpv[:, r, :], Qsrc[:, r, :], reads=[RQn], writes=[RQp])
            k.copy("dve" if r % 2 == 0 else "pool", KTpv[:, r, 64:64 + n], Ksrc[:, r, :], reads=[RKn], writes=[RKp])
        Vsrc = TM1d[:, g * 128:(g + 1) * 128].rearrange("(m r) c -> r m c", r=dil)
        for r in range(dil):
            if J > 1:
                k.dma("sp", VPv[:, r, 1:J, 0:128],
                      Vsrc[r, 64:64 + 128 * (J - 1), :].rearrange("(j p) c -> p j c", p=128),
                      reads=[RT1], writes=[RVP])
            k.dma("sp", VPv[64:128, r, 0, 0:128], Vsrc[r, 0:64, :], reads=[RT1], writes=[RVP])
            k.dma("sp", VPv[0:64, r, J, 0:128], Vsrc[r, n - 64:n, :], reads=[RT1], writes=[RVP])
        OGv = OG[g].rearrange("(j q r) c -> r q j c", q=128, r=dil)
        for r in range(dil):
            ost, Rost = Ost[r % 2]
            for j0 in range(0, J, 2):
                ps, Rps = pp.get()
                tp, Rtp = tmp[it % 2]
                pt, Rpt = PT[it % 2]
                it += 1
                for jj in range(2):
                    j = j0 + jj
                    q = QTpv[:, r, 128 * j:128 * j + 128]
                    k.mm(ps[:, 256 * jj:256 * jj + 128], KTpv[:, r, 128 * j:128 * j + 128], q, True, True,
                         reads=[RKp, RQp], writes=[Rps])
                    k.mm(ps[:, 256 * jj + 128:256 * jj + 256], KTpv[:, r, 128 * (j + 1):128 * (j + 2)], q, True, True,
                         reads=[RKp, RQp], writes=[Rps])
                    var = 0 if j == 0 else (2 if j == J - 1 else 1)
                    k.stt("dve", tp[:, 256 * jj:256 * jj + 256], ps[:, 256 * jj:256 * jj + 256], SCALE_A,
                          bm[:, g * 3 + var, :], ALU.mult, ALU.add, reads=[Rps, Rbm], writes=[Rtp])
                k.act(pt, tp, AF.Exp, reads=[Rtp], writes=[Rpt])
                for jj in range(2):
                    j = j0 + jj
                    po, Rpo = pp.get()
                    k.mm(po[:, 0:130], pt[:, 256 * jj:256 * jj + 128], VPv[:, r, j, :], True, False,
                         reads=[Rpt, RVP], writes=[Rpo])
                    k.mm(po[:, 0:130], pt[:, 256 * jj + 128:256 * jj + 256], VPv[:, r, j + 1, :], False, True,
                         reads=[Rpt, RVP], writes=[Rpo])
                    k.copy("act", ost[:, j, :], po[:, 0:130], reads=[Rpo], writes=[Rost])
            k.dma("sp", OGv[r, :, :, :], ost[:, 0:J, :], reads=[Rost], writes=[ROG], owner=Rost)
    k.barrier()
    ar.reset()
    NB = 8
    ogs = [ar.alloc(f"ogs{i}", [3, NB, 130], F32) for i in range(2)]
    gt = [ar.alloc(f"gt{i}", [NB, 128], F32) for i in range(2)]
    acc = [ar.alloc(f"accA{i}", [NB, 130], F32) for i in range(2)]
    rec = [ar.alloc(f"recA{i}", [NB, 1], F32) for i in range(2)]
    ob = [ar.alloc(f"obA{i}", [NB, 128], BF16) for i in range(2)]
    OGn = OG.rearrange("g (t p) c -> p g t c", p=128)
    Gn = TM2d[:, 0:128].rearrange("(t p) c -> p t c", p=128)
    On = brA.rearrange("(t p) c -> p t c", p=128)
    for c in range(S // 128 // NB):
        b = c % 2
        o, Ro = ogs[b]
        gg, Rg = gt[b]
        a, Ra = acc[b]
        rc, Rr = rec[b]
        oo, Roo = ob[b]
        ts_ = slice(c * NB, (c + 1) * NB)
        for g in range(3):
            k.dma("sp", o[:, g, :, :], OGn[:, g, ts_, :], reads=[ROG], writes=[Ro])
        k.dma("sp", gg, Gn[:, ts_, :], reads=[RT2], writes=[Rg])
        k.tt("dve", a, o[:, 0, :, :], o[:, 1, :, :], ALU.add, reads=[Ro], writes=[Ra])
        k.tt("dve", a, a, o[:, 2, :, :], ALU.add, reads=[Ro, Ra], writes=[Ra])
        k.op("dve", lambda e, rc=rc, a=a: e.reciprocal(rc, a[:, :, 128:129]), reads=[Ra], writes=[Rr])
        k.act(gg, gg, AF.Silu, reads=[Rg], writes=[Rg])
        k.tt("dve", a[:, :, 0:128], a[:, :, 0:128], rc.to_broadcast([128, NB, 128]), ALU.mult, reads=[Ra, Rr], writes=[Ra])
        k.tt("dve", oo, a[:, :, 0:128], gg, ALU.mult, reads=[Ra, Rg], writes=[Roo])
        k.dma("sp", On[:, ts_, :], oo, reads=[Roo], writes=[RbrA], owner=Roo)
    k.barrier()


def _t5_bucket_np(rel):
    nb = 16
    max_exact = 8
    rel = np.asarray(rel, np.int64)
    ret = (rel > 0).astype(np.int64) * nb
    n = np.abs(rel)
    nf = np.maximum(n, 1).astype(np.float32)
    large = max_exact + (np.log(nf / np.float32(max_exact)) / np.float32(math.log(1024 / max_exact))
                         * np.float32(nb - max_exact)).astype(np.int64)
    large = np.minimum(large, nb - 1)
    return ret + np.where(n < max_exact, n, large)


def _bma_index():
    kp = np.arange(128)[:, None]
    qf = np.arange(128)[None, :]
    idx = np.zeros((128, 9, 256), np.int64)
    for g, dil in enumerate((1, 4, 16)):
        relA = kp - 64 - qf
        relB = kp + 64 - qf
        bA = _t5_bucket_np(relA * dil)
        bB = _t5_bucket_np(relB * dil)
        vA = kp >= qf
        vB = kp <= qf
        for var in range(3):
            va = vA & (kp >= 64) if var == 0 else vA
            vb = vB & (kp < 64) if var == 2 else vB
            idx[:, g * 3 + var, 0:128] = np.where(va, bA, 32)
            idx[:, g * 3 + var, 128:256] = np.where(vb, bB, 32)
    return idx


def build_L1(S, stages=("P", "A", "B", "C"), debug=False):
    nc = bass.Bass("TRN2", target_bir_lowering=False)
    k = K(nc)
    ar = Arena(nc)
    pp = PsumPool(nc)
    d = {}

    def ext_in(name, shape, dt):
        d[name] = (nc.dram_tensor(name, shape, dt, kind="ExternalInput").ap(), DRes(name))

    def scratch(name, shape, dt, out=False):
        kind = "ExternalOutput" if out else "Internal"
        d[name] = (nc.dram_tensor(name, shape, dt, kind=kind).ap(), DRes(name))

    ext_in("hT", [2048, S], BF16)
    ext_in("W", [2048, 1920], F32)
    ext_in("WB", [2048, 384], F32)
    ext_in("convw", [3, 384], F32)
    ext_in("BMA", [128, 9, 256], F32)
    scratch("FMd", [8, 128, S], BF16, out=debug)
    scratch("TM1d", [S, 512], BF16, out=debug)
    scratch("TM2d", [S, 384], F32, out=debug)
    scratch("Bd", [S, 384], F32, out=debug)
    scratch("OG", [3, S, 130], F32)
    scratch("brA", [S, 128], BF16, out=True)
    ext_in("Zx", [33, 2 * S], F32)
    ext_in("tpos", [1, 2 * S], F32)
    ext_in("hw1", [33, 64], F32)
    ext_in("hw2", [64, 64], F32)
    ext_in("hw3", [64, 512], F32)
    ext_in("hyp", [64, 4], F32)
    ext_in("hsk", [128, 3], F32)
    ext_in("Jrev", [128, 128], BF16)
    scratch("hx", [2, 128, 2 * S], BF16)
    scratch("brB", [S, 128], BF16, out=True)
    ext_in("BC", [128, 17, 128], F32)
    ext_in("cpar", [128, 388], F32)
    scratch("brC", [S, 128], BF16, out=True)
    if "P" in stages:
        stage_P(k, ar, pp, S, d)
    if "A" in stages:
        stage_A(k, ar, pp, S, d)
    if "B" in stages:
        stage_B(k, ar, pp, S, d)
    if "C" in stages:
        stage_C(k, ar, pp, S, d)
    k.emit()
    return nc


SPL = [9216, 10240, 13312, 14336, 17408, 18432]


def hyena_consts(S):
    f32 = np.float32
    L = S
    t = np.linspace(0.0, 1.0, L, dtype=f32)
    w = (f32(2.0 * math.pi) * np.arange(L, dtype=f32)) / f32(L)
    fb = np.linspace(1e-4, 15, 16, dtype=f32)
    z = np.concatenate([t[:, None], np.cos(fb[None] * w[:, None]), -np.sin(fb[None] * w[:, None])], axis=-1).astype(f32)
    pos = np.minimum(np.abs(np.arange(2 * S) - (S - 1)), L - 1)
    Zx = np.ascontiguousarray(z[pos].T)
    tpos = np.ascontiguousarray(t[pos][None, :])
    max_decay = math.log(1e-2) / 0.3
    min_decay = math.log(1e-2) / 1.5
    deltas = np.abs(np.linspace(min_decay, max_decay, 1024, dtype=f32))
    return Zx, tpos, deltas


def l1_inputs(inp, l, hd, S, consts):
    Zx, tpos, deltas = consts
    w_in = inp["w_in"][l]

    def acol(g, t):
        return ((g * 3 + t) * 8 + hd) * 128

    cols = []
    for g in range(3):
        cols += [acol(g, 0), acol(g, 1)]
    cols += [SPL[3] + hd * 128, SPL[3] + 1024 + hd * 128]
    cols += [acol(0, 2), acol(1, 2), acol(2, 2), SPL[3] + 2048 + hd * 128]
    cols += [SPL[0] + hd * 128, SPL[2] + hd * 128, SPL[4] + hd * 128]
    W = np.concatenate([w_in[:, c:c + 128] for c in cols], 1)
    WB = np.concatenate([w_in[:, SPL[1] + j * 1024 + hd * 128:SPL[1] + j * 1024 + hd * 128 + 128] for j in range(3)], 1)
    convw = np.concatenate([inp["hy_conv"][l][:, j * 1024 + hd * 128:j * 1024 + hd * 128 + 128] for j in range(3)], 1)
    rb = inp["rel_bias"]
    bias_ext = np.concatenate([rb, np.full((1, 32), NEG_MASK, np.float32)], 0)
    idx = _bma_index()
    BMA = np.zeros((128, 9, 256), np.float32)
    for g in range(3):
        BMA[:, g * 3:(g + 1) * 3, :] = bias_ext[:, g * 8 + hd][idx[:, g * 3:(g + 1) * 3, :]]
    kp = np.arange(128)[:, None, None]
    dt = (8 - np.arange(17))[None, :, None]
    qf = np.arange(128)[None, None, :]
    BC = rb[:, 24 + hd][_t5_bucket_np(128 * dt + kp - qf)].astype(np.float32)
    lam_init = 0.8 - 0.6 * math.exp(-0.3 * l)
    row = np.concatenate([[rb[15, 24 + hd], rb[31, 24 + hd], np.float32(lam_init), np.float32(1.0 - lam_init)],
                          inp["diff_lam"][l].ravel(), inp["diff_g"][l]]).astype(np.float32)
    cpar = np.tile(row[None, :], (128, 1))
    hw3 = inp["hy_w3"][l].reshape(64, 2, 2, 1024)[:, :, :, hd * 128:(hd + 1) * 128].reshape(64, 512)
    hyp = np.stack([inp["hy_b1"][l], inp["hy_freq"][l][0], inp["hy_b2"][l], inp["hy_freq"][l][1]], 1)
    hsk = np.stack([inp["hy_skip"][l][0, hd * 128:(hd + 1) * 128], inp["hy_skip"][l][1, hd * 128:(hd + 1) * 128],
                    -deltas[hd * 128:(hd + 1) * 128]], 1)
    c = np.ascontiguousarray
    Jrev = np.eye(128, dtype=np.float32)[::-1].astype(ml_dtypes.bfloat16)
    return {"W": c(W), "WB": c(WB), "convw": c(convw), "BMA": BMA, "BC": c(BC), "cpar": cpar, "Jrev": c(Jrev),
            "Zx": Zx, "tpos": tpos, "hw1": c(inp["hy_w1"][l]), "hw2": c(inp["hy_w2"][l]), "hw3": c(hw3),
            "hyp": c(hyp.astype(np.float32)), "hsk": c(hsk.astype(np.float32))}


TWO_PI = 2.0 * math.pi


def stage_B(k, ar, pp, S, d):
    Bd, RBd = d["Bd"]
    TM2d, RT2 = d["TM2d"]
    Zx, RZx = d["Zx"]
    tpos, Rtp = d["tpos"]
    hw1, Rh1 = d["hw1"]
    hw2, Rh2 = d["hw2"]
    hw3, Rh3 = d["hw3"]
    hyp, Rhyp = d["hyp"]
    hsk, Rhsk = d["hsk"]
    hx, Rhx = d["hx"]
    brB, RbrB = d["brB"]
    NA = S // 128
    CH = 2048
    ar.reset()
    w1, Rw1 = ar.alloc("w1", [64], F32)
    w2, Rw2 = ar.alloc("w2", [64], F32)
    w3, Rw3 = ar.alloc("w3", [512], F32)
    hp, Rhp = ar.alloc("hp", [4], F32)
    sk, Rsk = ar.alloc("sk", [3], F32)
    k.dma("sp", w1[0:33, :], hw1, reads=[Rh1], writes=[Rw1])
    k.dma("sp", w2[0:64, :], hw2, reads=[Rh2], writes=[Rw2])
    k.dma("sp", w3[0:64, :], hw3, reads=[Rh3], writes=[Rw3])
    k.dma("sp", hp[0:64, :], hyp, reads=[Rhyp], writes=[Rhp])
    k.dma("sp", sk, hsk, reads=[Rhsk], writes=[Rsk])
    npi, Rnpi = ar.alloc("npi", [1], F32)
    k.memset("pool", npi, 0.5 * math.pi, writes=[Rnpi])
    hq, Rhq = ar.alloc("hq", [8], F32)
    for li in range(2):
        bcol, fcol = 2 * li, 2 * li + 1
        k.act(hq[0:64, 4 * li:4 * li + 1], hp[0:64, fcol:fcol + 1], AF.Copy, reads=[Rhp], writes=[Rhq], scale=0.5)
        k.tt("dve", hq[0:64, 4 * li + 1:4 * li + 2], hq[0:64, 4 * li:4 * li + 1], hp[0:64, bcol:bcol + 1], ALU.mult,
             reads=[Rhq, Rhp], writes=[Rhq])
        k.act(hq[0:64, 4 * li + 2:4 * li + 3], hp[0:64, fcol:fcol + 1], AF.Copy, reads=[Rhp], writes=[Rhq])
        k.tt("dve", hq[0:64, 4 * li + 3:4 * li + 4], hp[0:64, fcol:fcol + 1], hp[0:64, bcol:bcol + 1], ALU.mult,
             reads=[Rhp], writes=[Rhq])
    sA, RsA = ar.alloc("sA", [CH], F32)
    sB, RsB = ar.alloc("sB", [CH], F32)
    zc = [ar.alloc(f"zc{i}", [CH], F32) for i in range(2)]
    tpc = [ar.alloc(f"tpc{i}", [CH], F32) for i in range(2)]
    h1, Rh1s = ar.alloc("h1s", [CH], F32)
    h2, Rh2s = ar.alloc("h2s", [CH], F32)
    dec, Rdec = ar.alloc("dec", [CH], F32)
    hf, Rhf = ar.alloc("hf", [CH], F32)
    hb = [ar.alloc(f"hb{i}", [CH], BF16) for i in range(4)]
    nch = 2 * S // CH
    ctr = S - 1
    hbi = 0
    for ci in range(nch):
        z, Rz = zc[ci % 2]
        tp, Rtpc = tpc[ci % 2]
        c0 = ci * CH
        k.dma("sp", z[0:33, :], Zx[:, c0:c0 + CH], reads=[RZx], writes=[Rz])
        k.dma("sp", tp, tpos[:, c0:c0 + CH].partition_broadcast(128), reads=[Rtp], writes=[Rtpc])
        for (src, Rsrc, kk, wt, Rwt, li, dst, Rdst) in ((z, Rz, 33, w1, Rw1, 0, h1, Rh1s), (h1, Rh1s, 64, w2, Rw2, 1, h2, Rh2s)):
            for sc in range(CH // 512):
                ps, Rps = pp.get()
                cs = slice(sc * 512, (sc + 1) * 512)
                k.mm(ps[0:64, :], wt[0:kk, 0:64], src[0:kk, cs], True, True, reads=[Rwt, Rsrc], writes=[Rps])
                k.act(sA[0:64, cs], ps[0:64, :], AF.Sin, reads=[Rps, Rhq], writes=[RsA],
                      bias=hq[0:64, 4 * li + 1:4 * li + 2], scale=hq[0:64, 4 * li:4 * li + 1])
                k.act(sB[0:64, cs], ps[0:64, :], AF.Abs, reads=[Rps, Rhq], writes=[RsB],
                      bias=hq[0:64, 4 * li + 3:4 * li + 4], scale=hq[0:64, 4 * li + 2:4 * li + 3])
            k.act(sB[0:64, :], sB[0:64, :], AF.Sin, reads=[RsB, Rnpi], writes=[RsB], bias=npi[0:64, :], scale=-0.5)
            k.stt("dve", dst[0:64, :], sA[0:64, :], 2.0, sB[0:64, :], ALU.mult, ALU.mult, reads=[RsA, RsB], writes=[Rdst])
        k.act(dec, tp, AF.Exp, reads=[Rtpc, Rsk], writes=[Rdec], scale=sk[:, 2:3])
        for o in range(2):
            hbt, Rhb = hb[hbi % 4]
            hbi += 1
            for sc in range(CH // 512):
                ps, Rps = pp.get()
                lo = c0 + sc * 512
                cs = slice(sc * 512, (sc + 1) * 512)
                wf = w3[0:64, (o * 2 + 0) * 128:(o * 2 + 1) * 128]
                wb = w3[0:64, (o * 2 + 1) * 128:(o * 2 + 2) * 128]
                if lo + 512 <= ctr:
                    k.mm(ps[:, :], wb, h2[0:64, cs], True, True, reads=[Rw3, Rh2s], writes=[Rps])
                elif lo > ctr:
                    k.mm(ps[:, :], wf, h2[0:64, cs], True, True, reads=[Rw3, Rh2s], writes=[Rps])
                else:
                    assert lo + 511 == ctr
                    k.mm(ps[:, :], wb, h2[0:64, cs], True, False, reads=[Rw3, Rh2s], writes=[Rps])
                    k.mm(ps[:, 511:512], wf, h2[0:64, sc * 512 + 511:sc * 512 + 512], False, True,
                         reads=[Rw3, Rh2s], writes=[Rps])
                k.tt("dve", hf[:, cs], ps[:, :], dec[:, cs], ALU.mult, reads=[Rps, Rdec], writes=[Rhf])
                if lo <= ctr < lo + 512:
                    cc = sc * 512 + (ctr - lo)
                    k.tt("dve", hf[:, cc:cc + 1], hf[:, cc:cc + 1], sk[:, o:o + 1], ALU.add,
                         reads=[Rhf, Rsk], writes=[Rhf])
            k.copy("act", hbt, hf, reads=[Rhf], writes=[Rhb])
            k.dma("sp", hx[o, :, c0:c0 + CH], hbt, reads=[Rhb], writes=[Rhx], owner=Rhb)
    k.barrier()
    ar.reset()
    GW = 2 * S - 128
    G = [ar.alloc(f"G{i}", [GW], BF16) for i in range(3)]
    ZT, RZT = ar.alloc("ZT", [NA, 128], BF16)
    ZR, RZR = ar.alloc("ZR", [NA, 128], BF16)
    Jr, RJr = ar.alloc("Jr", [128], BF16)
    Jd, RJd = d["Jrev"]
    k.dma("sp", Jr, Jd, reads=[RJd], writes=[RJr])
    ZTf = ZT.rearrange("p a c -> p (a c)")
    ZRf = ZR.rearrange("p a c -> p (a c)")

    def reverse():
        for ch in range(NA * 128 // 512):
            psr, Rpsr = pp.get()
            k.mm(psr[:, :], Jr, ZTf[:, ch * 512:(ch + 1) * 512], True, True, reads=[RJr, RZT], writes=[Rpsr])
            k.copy("act" if ch % 2 == 0 else "dve", ZRf[:, ch * 512:(ch + 1) * 512], psr[:, :], reads=[Rpsr], writes=[RZR])

    XF, RXF = ar.alloc("XF", [NA, 128], F32)
    YT, RYT = ar.alloc("YT", [NA, 128], F32)
    Bv = Bd.rearrange("(a p) c -> p a c", p=128)
    k.dma("sp", XF, Bv[:, :, 0:128], reads=[RBd], writes=[RXF])
    k.copy("dve", ZT, XF, reads=[RXF], writes=[RZT])
    hxt = hx.tensor
    for o in range(2):
        reverse()
        for c in range(128):
            g, Rg = G[c % 3]
            src = bass.AP(hxt, (o * 128 + c) * 2 * S, [[1, 128], [1, GW]])
            k.dma("sp", g, src, reads=[Rhx], writes=[Rg])
            ps, Rps = pp.get()
            order = [0] + [dd for dd in range(-(NA - 1), NA) if dd != 0]
            for ii, dd in enumerate(order):
                xd = 128 * dd + S - 128
                a_lo, a_hi = max(0, -dd), min(NA, NA - dd)
                k.mm(ps[:, a_lo + dd:a_hi + dd], g[:, xd:xd + 128], ZR[:, a_lo:a_hi, c], ii == 0, ii == len(order) - 1,
                     reads=[Rg, RZR], writes=[Rps])
            k.copy("act" if c % 2 == 0 else "dve", YT[:, :, c], ps[:, 0:NA], reads=[Rps], writes=[RYT])
        k.dma("sp", XF, Bv[:, :, 128 * (o + 1):128 * (o + 2)], reads=[RBd], writes=[RXF])
        k.tt("dve", YT, YT, XF, ALU.mult, reads=[RYT, RXF], writes=[RYT])
        if o == 0:
            k.copy("dve", ZT, YT, reads=[RYT], writes=[RZT])
    k.dma("sp", XF, TM2d[:, 128:256].rearrange("(a p) c -> p a c", p=128), reads=[RT2], writes=[RXF])
    k.act(XF, XF, AF.Silu, reads=[RXF], writes=[RXF])
    k.tt("dve", ZT, YT, XF, ALU.mult, reads=[RYT, RXF], writes=[RZT])
    k.dma("sp", brB.rearrange("(a p) c -> p a c", p=128), ZT, reads=[RZT], writes=[RbrB], owner=RZT)
    k.barrier()


def stage_C(k, ar, pp, S, d):
    FMd, RFM = d["FMd"]
    TM1d, RT1 = d["TM1d"]
    TM2d, RT2 = d["TM2d"]
    BCd, RBCd = d["BC"]
    cpard, Rcpd = d["cpar"]
    brC, RbrC = d["brC"]
    NK = S // 128
    NG = S // 512
    ar.reset()
    CQ, RCQ = ar.alloc("CQ", [S], BF16)
    CK, RCK = ar.alloc("CK", [S], BF16)
    CV, RCV = ar.alloc("CV", [NK, 130], BF16)
    BC, RBC = ar.alloc("BC", [17 * 128], F32)
    cp, Rcp = ar.alloc("cpar", [388], F32)
    k.dma("sp", CQ, FMd[6, :, :], reads=[RFM], writes=[RCQ])
    k.dma("sp", CK, FMd[7, :, :], reads=[RFM], writes=[RCK])
    k.memset("pool", CV[:, :, 128:130], 1.0, writes=[RCV])
    k.dma("sp", CV[:, :, 0:128], TM1d[:, 384:512].rearrange("(t p) c -> p t c", p=128), reads=[RT1], writes=[RCV])
    k.dma("sp", BC, BCd.rearrange("p j q -> p (j q)"), reads=[RBCd], writes=[RBC])
    k.dma("sp", cp, cpard, reads=[Rcpd], writes=[Rcp])
    sm, Rsm = ar.alloc("smallC", [16], F32)
    t64, Rt64 = ar.alloc("t64", [2, 64], F32)
    k.tt("dve", t64[:, 0, :], cp[:, 4:68], cp[:, 68:132], ALU.mult, reads=[Rcp], writes=[Rt64])
    k.tt("dve", t64[:, 1, :], cp[:, 132:196], cp[:, 196:260], ALU.mult, reads=[Rcp], writes=[Rt64])
    k.op("dve", lambda e: e.reduce_sum(sm[:, 0:2], t64, AX.X), reads=[Rt64], writes=[Rsm])
    k.act(sm[:, 2:4], sm[:, 0:2], AF.Exp, reads=[Rsm], writes=[Rsm])
    k.tt("dve", sm[:, 4:5], sm[:, 2:3], sm[:, 3:4], ALU.subtract, reads=[Rsm], writes=[Rsm])
    k.tt("dve", sm[:, 4:5], sm[:, 4:5], cp[:, 2:3], ALU.add, reads=[Rsm, Rcp], writes=[Rsm])
    k.act(sm[:, 5:6], sm[:, 4:5], AF.Copy, reads=[Rsm], writes=[Rsm], scale=-1.0)
    k.memset("pool", sm[:, 6:7], 1e-6, writes=[Rsm])
    dgs, Rdgs = ar.alloc("dgs", [128], F32)
    k.act(dgs, cp[:, 260:388], AF.Copy, reads=[Rcp], writes=[Rdgs], scale=cp[:, 3:4])
    tmpb = [ar.alloc(f"tmpC{i}", [512], F32) for i in range(2)]
    PT = [ar.alloc(f"PTC{i}", [512], BF16) for i in range(4)]
    gt = [ar.alloc(f"gtC{i}", [4, 128], F32) for i in range(2)]
    ob = [ar.alloc(f"obC{i}", [4, 128], BF16) for i in range(2)]
    t2, Rt2 = ar.alloc("t2C", [128], F32)
    cpre, Rcpre = ar.alloc("cpre", [128], F32)
    sq, Rsq = ar.alloc("sqC", [128], F32)
    ep, Rep = ar.alloc("epC", [8], F32)
    accR = [Res(f"accC{a}") for a in range(8)]

    def acc_ap(a):
        return pp.t[5 + a // 3][:, (a % 3) * 160:(a % 3) * 160 + 130]

    Gv = TM2d[:, 256:384].rearrange("(t p) c -> p t c", p=128)
    Ov = brC.rearrange("(t p) c -> p t c", p=128)
    pp.n = 5
    pp.i = 0
    it = 0
    for Gq in range(NG):
        qt0 = 4 * Gq
        g, Rg = gt[Gq % 2]
        o, Ro = ob[Gq % 2]
        k.dma("sp", g, Gv[:, qt0:qt0 + 4, :], reads=[RT2], writes=[Rg])
        k.act(g, g, AF.Silu, reads=[Rg], writes=[Rg])
        for kt in range(NK):
            near = (qt0 - 5 <= kt <= qt0 + 8)
            for c in range(2):
                ps, Rps = pp.get()
                k.mm(ps[:, :], CK[64 * c:64 * c + 64, kt * 128:(kt + 1) * 128], CQ[64 * c:64 * c + 64, Gq * 512:(Gq + 1) * 512],
                     True, True, reads=[RCK, RCQ], writes=[Rps])
                pt, Rpt = PT[it % 4]
                if near:
                    tb, Rtb = tmpb[it % 2]
                    j0 = 8 - (kt - qt0)
                    k.stt("dve", tb, ps[:, :], SCALE_C, BC[:, j0 * 128:(j0 + 4) * 128], ALU.mult, ALU.add,
                          reads=[Rps, RBC], writes=[Rtb])
                    k.act(pt, tb, AF.Exp, reads=[Rtb], writes=[Rpt])
                else:
                    bcol = cp[:, 0:1] if kt < qt0 else cp[:, 1:2]
                    k.act(pt, ps[:, :], AF.Exp, reads=[Rps, Rcp], writes=[Rpt], bias=bcol, scale=SCALE_C)
                it += 1
                for qi in range(4):
                    a = c * 4 + qi
                    k.mm(acc_ap(a), pt[:, qi * 128:(qi + 1) * 128], CV[:, kt, :], kt == 0 and a % 3 == 0, kt == NK - 1,
                         reads=[Rpt, RCV], writes=[accR[a]])
        for qi in range(4):
            O1, O2 = acc_ap(qi), acc_ap(4 + qi)
            R1, R2 = accR[qi], accR[4 + qi]
            k.op("dve", lambda e, O1=O1: e.reciprocal(ep[:, 0:1], O1[:, 128:129]), reads=[R1], writes=[Rep])
            k.op("dve", lambda e, O2=O2: e.reciprocal(ep[:, 1:2], O2[:, 128:129]), reads=[R2], writes=[Rep])
            k.tt("dve", ep[:, 1:2], ep[:, 1:2], sm[:, 5:6], ALU.mult, reads=[Rep, Rsm], writes=[Rep])
            k.act(t2, O2[:, 0:128], AF.Copy, reads=[R2, Rep], writes=[Rt2], scale=ep[:, 1:2])
            k.stt("dve", cpre, O1[:, 0:128], ep[:, 0:1], t2, ALU.mult, ALU.add, reads=[R1, Rep, Rt2], writes=[Rcpre])
            k.tt("dve", sq, cpre, cpre, ALU.mult, reads=[Rcpre], writes=[Rsq])
            k.op("dve", lambda e: e.reduce_sum(ep[:, 2:3], sq, AX.X), reads=[Rsq], writes=[Rep])
            k.act(ep[:, 3:4], ep[:, 2:3], AF.Sqrt, reads=[Rep, Rsm], writes=[Rep], bias=sm[:, 6:7], scale=1.0 / 128.0)
            k.op("dve", lambda e: e.reciprocal(ep[:, 4:5], ep[:, 3:4]), reads=[Rep], writes=[Rep])
            k.stt("dve", cpre, cpre, ep[:, 4:5], dgs, ALU.mult, ALU.mult, reads=[Rcpre, Rep, Rdgs], writes=[Rcpre])
            k.tt("dve", o[:, qi, :], cpre, g[:, qi, :], ALU.mult, reads=[Rcpre, Rg], writes=[Ro])
        k.dma("sp", Ov[:, qt0:qt0 + 4, :], o, reads=[Ro], writes=[RbrC], owner=Ro)
    pp.n = 8
    pp.i = 0
    k.barrier()


def _rmsnorm_tile(k, xt, Rx, gt, Rgt, sq, Rsq, ep, Rep, out, Rout, D):
    k.tt("dve", sq, xt, xt, ALU.mult, reads=[Rx], writes=[Rsq])
    k.op("dve", lambda e: e.reduce_sum(ep[:, 0:1], sq, AX.X), reads=[Rsq], writes=[Rep])
    k.act(ep[:, 1:2], ep[:, 0:1], AF.Sqrt, reads=[Rep], writes=[Rep], bias=ep[:, 3:4], scale=1.0 / D)
    k.op("dve", lambda e: e.reciprocal(ep[:, 2:3], ep[:, 1:2]), reads=[Rep], writes=[Rep])
    k.stt("dve", out, xt, ep[:, 2:3], gt, ALU.mult, ALU.mult, reads=[Rx, Rep, Rgt], writes=[Rout])


def build_L0(T):
    nc = bass.Bass("TRN2", target_bir_lowering=False)
    k = K(nc)
    ar = Arena(nc)
    x = nc.dram_tensor("x", [T, 2048], F32, kind="ExternalInput").ap()
    g = nc.dram_tensor("g", [1, 2048], F32, kind="ExternalInput").ap()
    hn = nc.dram_tensor("hn", [T, 2048], BF16, kind="ExternalOutput").ap()
    Rx, Rg, Rhn = DRes("x"), DRes("g"), DRes("hn")
    gt, Rgt = ar.alloc("gt", [2048], F32)
    k.dma("sp", gt, g.partition_broadcast(128), reads=[Rg], writes=[Rgt])
    xt = [ar.alloc(f"xt{i}", [2048], F32) for i in range(2)]
    ot = [ar.alloc(f"ot{i}", [2048], BF16) for i in range(2)]
    sq, Rsq = ar.alloc("sq", [2048], F32)
    ep, Rep = ar.alloc("ep", [4], F32)
    k.memset("pool", ep[:, 3:4], 1e-6, writes=[Rep])
    for t in range(T // 128):
        xx, Rxx = xt[t % 2]
        oo, Roo = ot[t % 2]
        k.dma("sp", xx, x[t * 128:(t + 1) * 128, :], reads=[Rx], writes=[Rxx])
        _rmsnorm_tile(k, xx, Rxx, gt, Rgt, sq, Rsq, ep, Rep, oo, Roo, 2048.0)
        k.dma("sp", hn[t * 128:(t + 1) * 128, :], oo, reads=[Roo], writes=[Rhn], owner=Roo)
    k.emit()
    return nc


def build_L23(TT, last, T=1024):
    nc = bass.Bass("TRN2", target_bir_lowering=False)
    k = K(nc)
    ar = Arena(nc)
    pp = PsumPool(nc)
    ODT = F32 if last else BF16
    x = nc.dram_tensor("x", [TT, 2048], F32, kind="ExternalInput").ap()
    hT = nc.dram_tensor("hT", [2048, TT], BF16, kind="ExternalInput").ap()
    brT = nc.dram_tensor("brT", [3072, TT], BF16, kind="ExternalInput").ap()
    Wm = nc.dram_tensor("Wm", [2048, 6144], F32, kind="ExternalInput").ap()
    mbT = nc.dram_tensor("mbT", [128, 48], F32, kind="ExternalInput").ap()
    Wp = nc.dram_tensor("Wp", [3072, 2048], F32, kind="ExternalInput").ap()
    Wo = nc.dram_tensor("Wo", [2048, 2048], F32, kind="ExternalInput").ap()
    g = nc.dram_tensor("g", [1, 2048], F32, kind="ExternalInput").ap()
    xn = nc.dram_tensor("xn", [TT, 2048], F32, kind="ExternalOutput").ap()
    hn = nc.dram_tensor("hn", [TT, 2048], ODT, kind="ExternalOutput").ap()
    RD = {n: DRes(n) for n in ("x", "hT", "brT", "Wm", "mbT", "Wp", "Wo", "g", "xn", "hn")}
    NTG = T // 512
    Wmv = Wm.rearrange("(kc p) (n d) -> p kc n d", p=128, n=3)
    Wpv = Wp.rearrange("(kc p) d -> p kc d", p=128)
    Wov = Wo.rearrange("(kc p) e -> p kc e", p=128)
    hTv = hT.rearrange("(kc p) t -> p kc t", p=128)
    brTv = brT.rearrange("(kc p) t -> p kc t", p=128)
    hs, Rhs = ar.alloc("hs", [16, T], BF16)
    bs, Rbs = ar.alloc("bs", [24, T], BF16)
    yT, RyT = ar.alloc("yT", [16, T], BF16)
    p1_end = ar.off
    mb, Rmb = ar.alloc("mb", [48], F32)
    wm = [ar.alloc(f"wm{i}", [16, 3, 128], BF16) for i in range(2)]
    wp = [ar.alloc(f"wp{i}", [24, 128], BF16) for i in range(2)]
    gate = [ar.alloc(f"gate{i}", [512], F32) for i in range(2)]
    yacc, Ryacc = ar.alloc("yacc", [512], F32)
    tmpy, Rtmpy = ar.alloc("tmpy", [512], F32)
    RyTs = [Res(f"yT{dc}") for dc in range(16)]
    ar.off = 0
    wo, _ = ar.alloc("wo", [16, 2048], BF16)
    assert ar.off <= (16 + 24) * T * 2
    ar.off = p1_end
    Rwoq = [Res(f"wo{q}") for q in range(4)]
    gt, Rgt = ar.alloc("gt", [2048], F32)
    xt = [ar.alloc(f"xt{i}", [2048], F32) for i in range(2)]
    ot = [ar.alloc(f"ot{i}", [2048], ODT) for i in range(2)]
    sq, Rsq = ar.alloc("sq", [2048], F32)
    ep, Rep = ar.alloc("ep", [4], F32)
    it = 0
    for blk in range(TT // T):
        tb = slice(blk * T, (blk + 1) * T)
        k.dma("sp", hs, hTv[:, :, tb], reads=[RD["hT"]], writes=[Rhs])
        k.dma("sp", bs, brTv[:, :, tb], reads=[RD["brT"]], writes=[Rbs])
        k.dma("sp", mb, mbT, reads=[RD["mbT"]], writes=[Rmb])
        for dc in range(16):
            wmt, Rwm = wm[dc % 2]
            wpt, Rwp = wp[dc % 2]
            for n in range(3):
                k.dma("pool", wmt[:, :, n, :], Wmv[:, :, n, dc * 128:(dc + 1) * 128], reads=[RD["Wm"]], writes=[Rwm])
            k.dma("pool", wpt, Wpv[:, :, dc * 128:(dc + 1) * 128], reads=[RD["Wp"]], writes=[Rwp])
            for tg in range(NTG):
                ts_ = slice(tg * 512, (tg + 1) * 512)
                for n in range(3):
                    psm, Rpsm = pp.get()
                    for kc in range(16):
                        k.mm(psm[:, :], wmt[:, kc, n, :], hs[:, kc, ts_], kc == 0, kc == 15, reads=[Rwm, Rhs], writes=[Rpsm])
                    psp, Rpsp = pp.get()
                    for wc in range(8):
                        k.mm(psp[:, :], wpt[:, n * 8 + wc, :], bs[:, n * 8 + wc, ts_], wc == 0, wc == 7,
                             reads=[Rwp, Rbs], writes=[Rpsp])
                    gt_, Rgt_ = gate[it % 2]
                    it += 1
                    k.act(gt_, psm[:, :], AF.Sigmoid, reads=[Rpsm, Rmb], writes=[Rgt_], bias=mb[:, n * 16 + dc:n * 16 + dc + 1])
                    if n == 0:
                        k.tt("dve", yacc, gt_, psp[:, :], ALU.mult, reads=[Rgt_, Rpsp], writes=[Ryacc])
                    else:
                        k.tt("dve", tmpy, gt_, psp[:, :], ALU.mult, reads=[Rgt_, Rpsp], writes=[Rtmpy])
                        if n == 1:
                            k.tt("dve", yacc, yacc, tmpy, ALU.add, reads=[Ryacc, Rtmpy], writes=[Ryacc])
                        else:
                            k.tt("dve", yT[:, dc, ts_], yacc, tmpy, ALU.add, reads=[Ryacc, Rtmpy], writes=[RyTs[dc]])
        k.barrier()
        for q in range(4):
            k.dma("pool", wo[:, 4 * q:4 * q + 4, :], Wov[:, 4 * q:4 * q + 4, :], reads=[RD["Wo"]], writes=[Rwoq[q]])
        k.dma("sp", gt, g.partition_broadcast(128), reads=[RD["g"]], writes=[Rgt])
        k.memset("pool", ep[:, 3:4], 1e-6, writes=[Rep])
        for t in range(T // 128):
            xx, Rxx = xt[t % 2]
            oo, Roo = ot[t % 2]
            r0 = blk * T + t * 128
            k.dma("sp", xx, x[r0:r0 + 128, :], reads=[RD["x"]], writes=[Rxx])
            for ec in range(4):
                ps, Rps = pp.get()
                for kc in range(16):
                    k.mm(ps[:, :], yT[:, kc, t * 128:(t + 1) * 128], wo[:, kc, ec * 512:(ec + 1) * 512], kc == 0, kc == 15,
                         reads=[RyTs[kc], Rwoq[kc // 4]], writes=[Rps])
                k.tt("dve", xx[:, ec * 512:(ec + 1) * 512], xx[:, ec * 512:(ec + 1) * 512], ps[:, :], ALU.add,
                     reads=[Rxx, Rps], writes=[Rxx])
            k.dma("sp", xn[r0:r0 + 128, :], xx, reads=[Rxx], writes=[RD["xn"]], owner=Rxx)
            _rmsnorm_tile(k, xx, Rxx, gt, Rgt, sq, Rsq, ep, Rep, oo, Roo, 2048.0)
            k.dma("sp", hn[r0:r0 + 128, :], oo, reads=[Roo], writes=[RD["hn"]], owner=Roo)
        k.barrier()
    k.emit()
    return nc


_PROGS = {}


def _prog(name, fn):
    if name not in _PROGS:
        _PROGS[name] = fn()
    return _PROGS[name]


def kernel(x, norm_g, final_g, w_in, merge_b, rel_bias, hy_conv, hy_w1, hy_b1, hy_freq, hy_w2, hy_b2, hy_w3,
           hy_skip, diff_lam, diff_g, w_proj, w_out):
    NCORE = 8
    S, D, DEPTH = 8192, 2048, 4
    T = S // NCORE
    cores = list(range(NCORE))
    c = np.ascontiguousarray
    inp = dict(w_in=np.asarray(w_in), rel_bias=np.asarray(rel_bias), hy_conv=np.asarray(hy_conv), hy_w1=np.asarray(hy_w1),
               hy_b1=np.asarray(hy_b1), hy_freq=np.asarray(hy_freq), hy_w2=np.asarray(hy_w2), hy_b2=np.asarray(hy_b2),
               hy_w3=np.asarray(hy_w3), hy_skip=np.asarray(hy_skip), diff_lam=np.asarray(diff_lam), diff_g=np.asarray(diff_g))
    norm_g = np.asarray(norm_g, np.float32)
    final_g = np.asarray(final_g, np.float32)
    merge_b = np.asarray(merge_b, np.float32)
    w_proj = np.asarray(w_proj)
    w_out = np.asarray(w_out)
    xfull = c(np.asarray(x)[0])
    xs = [c(xfull[i * T:(i + 1) * T, :]) for i in range(NCORE)]
    consts = hyena_consts(S)
    nc0 = _prog("L0", lambda: build_L0(T))
    res = run_bass_kernel_spmd(nc0, [{"x": xs[i], "g": c(norm_g[0][None, :])} for i in range(NCORE)], core_ids=cores)
    hn = [res.results[i]["hn"] for i in range(NCORE)]
    out = None
    for l in range(DEPTH):
        hT = c(np.concatenate(hn, axis=0).T)
        nc1 = _prog("L1", lambda: build_L1(S))
        maps = []
        for hd in range(NCORE):
            m = l1_inputs(inp, l, hd, S, consts)
            m["hT"] = hT
            maps.append(m)
        res = run_bass_kernel_spmd(nc1, maps, core_ids=cores)
        br = np.stack([np.concatenate([res.results[hd][nm] for hd in range(NCORE)], axis=1) for nm in ("brA", "brB", "brC")], 0)
        del res
        last = (l == DEPTH - 1)
        nc2 = _prog("L23_last" if last else "L23", lambda: build_L23(T, last))
        gnext = final_g if last else norm_g[l + 1]
        Wm = c(inp["w_in"][l][:, SPL[5]:])
        mbT = c(merge_b[l].reshape(3, 16, 128).transpose(2, 0, 1).reshape(128, 48))
        Wp = c(w_proj[l].reshape(3072, 2048))
        Wo = c(w_out[l])
        gn = c(gnext[None, :])
        maps = []
        for i in range(NCORE):
            tk = slice(i * T, (i + 1) * T)
            maps.append({"x": c(xfull[tk]), "hT": c(hT[:, tk]), "brT": c(br[:, tk, :].transpose(0, 2, 1).reshape(3072, T)),
                         "Wm": Wm, "mbT": mbT, "Wp": Wp, "Wo": Wo, "g": gn})
        res = run_bass_kernel_spmd(nc2, maps, core_ids=cores)
        xfull = np.concatenate([res.results[i]["xn"] for i in range(NCORE)], axis=0)
        hn = [res.results[i]["hn"] for i in range(NCORE)]
        del res, maps
    out = np.concatenate(hn, axis=0).astype(np.float32, copy=False)[None]
    return out
```
